# Optimizing a Trainium2 kernel written in Bass

```python
import jax, jax.numpy as jnp
from jax import lax
import numpy as np

D_MODEL = 1024
BATCH = 4
SEQ = 4096
DEPTH = 2

GRID_W = 64
CTX_LEN = 256
EPS = 1e-6
GLA_HEADS = 4
GLA_DK = D_MODEL // (2 * GLA_HEADS)
GLA_DV = D_MODEL // GLA_HEADS
GLA_QK = GLA_HEADS * GLA_DK
GLA_V = GLA_HEADS * GLA_DV
GLA_LOWRANK = 16
GLA_TAU = 16.0
GLA_CHUNK = 64
SC_WIDTH = D_MODEL
SC_CONV = 3
RG_WIDTH = 2 * D_MODEL
RG_BLOCKS = 16
RG_BLOCK_W = RG_WIDTH // RG_BLOCKS
RG_C = 8.0
RG_CONV = 4

EVEN_SIZES = (GLA_QK, GLA_QK, GLA_V, GLA_V, GLA_LOWRANK, GLA_LOWRANK,
              SC_WIDTH, SC_WIDTH, SC_WIDTH, SC_WIDTH)
EVEN_IN = 2 * GLA_QK + 2 * GLA_V + 2 * GLA_LOWRANK + 4 * SC_WIDTH
EVEN_OUT = GLA_V + SC_WIDTH

kernel_name = "hybrid_gla_shortconv_rglru_prefix_dit"


def _rmsnorm(x, g):
    xf = x.astype(jnp.float32)
    y = xf * lax.rsqrt(jnp.mean(xf * xf, axis=-1, keepdims=True) + EPS)
    return (y * g.astype(jnp.float32)).astype(x.dtype)


def _modulate(x, g, shift, scale):
    return _rmsnorm(x, g) * (1 + scale) + shift


def _split_cols(p, sizes):
    out, off = [], 0
    for s in sizes:
        out.append(p[..., off:off + s])
        off += s
    return out


def _conv3_centered(x, w):
    n = x.shape[-2]
    pad = [(0, 0)] * (x.ndim - 2) + [(1, 1), (0, 0)]
    xp = jnp.pad(x, pad)
    return w[0] * xp[..., 0:n, :] + w[1] * xp[..., 1:n + 1, :] + w[2] * xp[..., 2:n + 2, :]


def _conv4_directional(x, w, b, reverse):
    if reverse:
        x = jnp.flip(x, 1)
    length = x.shape[1]
    xp = jnp.pad(x, ((0, 0), (RG_CONV - 1, 0), (0, 0)))
    y = b
    for j in range(RG_CONV):
        y = y + w[j] * xp[:, j:j + length]
    return jnp.flip(y, 1) if reverse else y


def _gla_direction(q, k, v, log_g, s0):
    bsz, length, heads, _ = q.shape
    dv = v.shape[-1]
    n = length // GLA_CHUNK

    def chunks(t):
        return t.reshape(bsz, n, GLA_CHUNK, heads, t.shape[-1])

    q, k, v, log_g = chunks(q), chunks(k), chunks(v), chunks(log_g)
    b = jnp.cumsum(log_g, axis=2)
    b_last = b[:, :, -1:]
    q_in = q * jnp.exp(b)
    k_in = k * jnp.exp(-b)
    scores = jnp.einsum("bnthk,bnshk->bnhts", q_in, k_in)
    mask = jnp.tril(jnp.ones((GLA_CHUNK, GLA_CHUNK), dtype=bool))
    scores = jnp.where(mask, scores, 0.0)
    o_intra = jnp.einsum("bnhts,bnshv->bnthv", scores, v)
    k_end = k * jnp.exp(b_last - b)
    incr = jnp.einsum("bnshk,bnshv->bnhkv", k_end, v)
    decay = jnp.exp(b_last[:, :, 0])

    def step(state, inp):
        d, u = inp
        return d[..., None] * state + u, state

    s_final, s_prev = lax.scan(step, s0, (jnp.moveaxis(decay, 1, 0), jnp.moveaxis(incr, 1, 0)))
    o_inter = jnp.einsum("bnthk,nbhkv->bnthv", q_in, s_prev)
    return (o_intra + o_inter).reshape(bsz, length, heads, dv), s_final


def _even_branch(h, s0_f, s0_b, on_grid, need_out, w_in, w_a2, b_a2, gla_g, conv_w):
    f32 = jnp.float32
    bsz, length, _ = h.shape
    p = h @ w_in
    q, k, v, g_a, a_f, a_b, c_b, c_c, c_x, g_b = _split_cols(p, EVEN_SIZES)
    q = q.reshape(bsz, length, GLA_HEADS, GLA_DK).astype(f32) * (GLA_DK ** -0.5)
    k = k.reshape(bsz, length, GLA_HEADS, GLA_DK).astype(f32)
    v = v.reshape(bsz, length, GLA_HEADS, GLA_DV).astype(f32)

    def log_gate(a_lr, d):
        z = a_lr.astype(f32) @ w_a2[d].astype(f32) + b_a2[d].astype(f32)
        return (jax.nn.log_sigmoid(z) / GLA_TAU).reshape(bsz, length, GLA_HEADS, GLA_DK)

    o_f, s_f = _gla_direction(q, k, v, log_gate(a_f, 0), s0_f)
    fl = lambda t: jnp.flip(t, 1)
    o_b, s_b = _gla_direction(fl(q), fl(k), fl(v), fl(log_gate(a_b, 1)), s0_b)
    if not need_out:
        return None, s_f, s_b
    o = _rmsnorm(o_f + fl(o_b), gla_g).reshape(bsz, length, GLA_V).astype(h.dtype)
    o = o * jax.nn.silu(g_a)
    z = c_c * c_x
    if on_grid:
        rows = length // GRID_W
        zc = _conv3_centered(z.reshape(bsz, rows, GRID_W, SC_WIDTH), conv_w).reshape(bsz, length, SC_WIDTH)
    else:
        zc = _conv3_centered(z, conv_w)
    y = c_b * zc * jax.nn.silu(g_b)
    return jnp.concatenate([o, y], axis=-1), s_f, s_b


def _lin_combine(e1, e2):
    a1, b1 = e1
    a2, b2 = e2
    return a1 * a2, a2 * b1 + b2


def _rglru_direction(xr, conv_w, conv_b, w_a, b_a, w_x, b_x, lam, h0, reverse):
    f32 = jnp.float32
    xc = _conv4_directional(xr.astype(f32), conv_w.astype(f32), conv_b.astype(f32), reverse)
    bsz, length, _ = xc.shape
    blk = xc.reshape(bsz, length, RG_BLOCKS, RG_BLOCK_W)
    r = jax.nn.sigmoid(jnp.einsum("blni,nij->blnj", blk, w_a.astype(f32)) + b_a.astype(f32))
    i = jax.nn.sigmoid(jnp.einsum("blni,nij->blnj", blk, w_x.astype(f32)) + b_x.astype(f32))
    r = r.reshape(bsz, length, RG_WIDTH)
    i = i.reshape(bsz, length, RG_WIDTH)
    log_a = -RG_C * r * jax.nn.softplus(-lam.astype(f32))
    a = jnp.exp(log_a)
    u = jnp.sqrt(-jnp.expm1(2.0 * log_a)) * (i * xc)
    a_cum, hs = lax.associative_scan(_lin_combine, (a, u), reverse=reverse, axis=1)
    hs = hs + a_cum * h0[:, None, :]
    fin = hs[:, 0] if reverse else hs[:, -1]
    return hs, fin


def _odd_branch(h, h0_f, h0_b, need_out, w_in, conv_w, conv_b, w_a, b_a, w_x, b_x, lam):
    xr = h @ w_in[:, :RG_WIDTH]
    h_f, fin_f = _rglru_direction(xr, conv_w[0], conv_b[0], w_a[0], b_a[0], w_x[0], b_x[0], lam[0], h0_f, False)
    h_b, fin_b = _rglru_direction(xr, conv_w[1], conv_b[1], w_a[1], b_a[1], w_x[1], b_x[1], lam[1], h0_b, True)
    if not need_out:
        return None, fin_f, fin_b
    gate = h @ w_in[:, RG_WIDTH:]
    y = (h_f + h_b).astype(h.dtype) * jax.nn.silu(gate)
    return y, fin_f, fin_b


def setup_inputs(seed: int = 0) -> dict:
    key = jax.random.key(seed)
    ks = jax.random.split(key, 32)
    n_even = (DEPTH + 1) // 2
    n_odd = DEPTH // 2
    nrm = lambda k, shape, s: jax.random.normal(k, shape, jnp.float32) * s
    a8 = jax.random.uniform(ks[22], (n_odd, 2, RG_WIDTH), jnp.float32, minval=0.9, maxval=0.999)
    s = a8 ** (1.0 / RG_C)
    lam = jnp.log(s) - jnp.log1p(-s)
    return {
        "x": nrm(ks[0], (BATCH, SEQ, D_MODEL), 1.0),
        "c": nrm(ks[1], (BATCH, D_MODEL), 1.0),
        "ctx": nrm(ks[2], (BATCH, CTX_LEN, D_MODEL), 1.0),
        "c_ctx": nrm(ks[3], (D_MODEL,), 1.0),
        "norm_g": 1.0 + nrm(ks[4], (DEPTH, D_MODEL), 0.02),
        "w_mod": nrm(ks[5], (DEPTH, D_MODEL, 3 * D_MODEL), 0.5 * D_MODEL ** -0.5),
        "b_mod": nrm(ks[6], (DEPTH, 3 * D_MODEL), 0.02),
        "e_w_in": nrm(ks[7], (n_even, D_MODEL, EVEN_IN), D_MODEL ** -0.5),
        "e_w_a2": nrm(ks[8], (n_even, 2, GLA_LOWRANK, GLA_QK), GLA_LOWRANK ** -0.5),
        "e_b_a2": nrm(ks[9], (n_even, 2, GLA_QK), 0.1),
        "e_gla_g": 1.0 + nrm(ks[10], (n_even, GLA_DV), 0.02),
        "e_conv_w": nrm(ks[11], (n_even, SC_CONV, SC_WIDTH), SC_CONV ** -0.5),
        "e_w_out": nrm(ks[12], (n_even, EVEN_OUT, D_MODEL), EVEN_OUT ** -0.5),
        "o_w_in": nrm(ks[13], (n_odd, D_MODEL, 2 * RG_WIDTH), D_MODEL ** -0.5),
        "o_conv_w": nrm(ks[14], (n_odd, 2, RG_CONV, RG_WIDTH), RG_CONV ** -0.5),
        "o_conv_b": nrm(ks[15], (n_odd, 2, RG_WIDTH), 0.02),
        "o_w_a": nrm(ks[16], (n_odd, 2, RG_BLOCKS, RG_BLOCK_W, RG_BLOCK_W), RG_BLOCK_W ** -0.5),
        "o_b_a": nrm(ks[17], (n_odd, 2, RG_BLOCKS, RG_BLOCK_W), 0.02),
        "o_w_x": nrm(ks[18], (n_odd, 2, RG_BLOCKS, RG_BLOCK_W, RG_BLOCK_W), RG_BLOCK_W ** -0.5),
        "o_b_x": nrm(ks[19], (n_odd, 2, RG_BLOCKS, RG_BLOCK_W), 0.02),
        "o_lam": lam,
        "o_w_out": nrm(ks[20], (n_odd, RG_WIDTH, D_MODEL), RG_WIDTH ** -0.5),
        "final_g": 1.0 + nrm(ks[21], (D_MODEL,), 0.02),
    }


def reference(x, c, ctx, c_ctx, norm_g, w_mod, b_mod, e_w_in, e_w_a2, e_b_a2, e_gla_g,
              e_conv_w, e_w_out, o_w_in, o_conv_w, o_conv_b, o_w_a, o_b_a, o_w_x, o_b_x,
              o_lam, o_w_out, final_g):
    bsz = x.shape[0]
    s_c = jax.nn.silu(c)
    s_cc = jax.nn.silu(c_ctx)
    x_ctx = ctx
    for li in range(DEPTH):
        last = li == DEPTH - 1
        mod = s_c @ w_mod[li] + b_mod[li]
        shift, scale, gate = jnp.split(mod[:, None, :], 3, axis=-1)
        mod_c = s_cc @ w_mod[li] + b_mod[li]
        shift_c, scale_c, gate_c = jnp.split(mod_c, 3, axis=-1)
        h = _modulate(x, norm_g[li], shift, scale)
        h_c = _modulate(x_ctx, norm_g[li], shift_c, scale_c)
        if li % 2 == 0:
            j = li // 2
            prm = (e_w_in[j], e_w_a2[j], e_b_a2[j], e_gla_g[j], e_conv_w[j])
            s0 = jnp.zeros((bsz, GLA_HEADS, GLA_DK, GLA_DV), jnp.float32)
            y_c, s_f, s_b = _even_branch(h_c, s0, s0, False, not last, *prm)
            y, _, _ = _even_branch(h, s_f, s_b, True, True, *prm)
            w_out = e_w_out[j]
        else:
            j = li // 2
            prm = (o_w_in[j], o_conv_w[j], o_conv_b[j], o_w_a[j], o_b_a[j], o_w_x[j], o_b_x[j], o_lam[j])
            h0 = jnp.zeros((bsz, RG_WIDTH), jnp.float32)
            y_c, h_f, h_b = _odd_branch(h_c, h0, h0, not last, *prm)
            y, _, _ = _odd_branch(h, h_f, h_b, True, *prm)
            w_out = o_w_out[j]
        x = x + gate * (y @ w_out)
        if not last:
            x_ctx = x_ctx + gate_c * (y_c @ w_out)
    return _rmsnorm(x, final_g)
```

```python
import numpy as np
from contextlib import ExitStack
import concourse.bass as bass
import concourse.mybir as mybir
from concourse.bass_utils import run_bass_kernel_spmd

F32 = mybir.dt.float32
BF16 = mybir.dt.bfloat16
AF = mybir.ActivationFunctionType
ALU = mybir.AluOpType

D = 1024
KC = 8
EPS = 1e-6
EVEN_IN = 7200
C_Q, C_K, C_V, C_GA, C_AF, C_AB, C_CB, C_CC, C_CX, C_GB = 0, 512, 1024, 2048, 3072, 3088, 3104, 4128, 5152, 6176


class Buf:
    __slots__ = ("name", "w", "r", "excl")

    def __init__(self, name="", excl=False):
        self.name = name
        self.w = None
        self.r = {}
        self.excl = excl


class Cut(Exception):
    pass


class Emitter:
    ENGS = ("pe", "act", "dve", "pool", "sp")
    max_inst = None
    marks = None

    def mark(self, name):
        if self.marks is not None:
            self.marks.append((name, self.n_inst, dict(self.raw)))

    def _chk(self):
        return self.max_inst is not None and self.n_inst >= self.max_inst

    def __init__(self, nc, n_chan=24):
        self.nc = nc
        self.prog = {e: [] for e in self.ENGS}
        self.sems, self.cnt, self.mult = {}, {}, {}
        self._ctx = []
        for e in self.ENGS:
            self._newunit(e, 1)
        self.chans = []
        for i in range(n_chan):
            u = "ch%d" % i
            self._newunit(u, 16)
            self.chans.append(u)
        self.seen = {e: {} for e in self.ENGS}
        self.chan_rr = 0
        self.n_inst = 0
        self.n_wait = 0
        self.raw = {e: 0 for e in self.ENGS}

    def _newunit(self, u, mult):
        cm = self.nc.semaphore("s_" + u)
        s = cm.__enter__()
        self._ctx.append(cm)
        self.sems[u] = s
        self.cnt[u] = 0
        self.mult[u] = mult

    def _deps(self, eng, reads, writes):
        deps = {}

        def add(uc, same_ok):
            if uc is None:
                return
            u, c = uc
            if u == eng and same_ok:
                return
            if deps.get(u, 0) < c:
                deps[u] = c

        pe = (eng == "pe")
        for b in reads:
            add(b.w, pe)
            if b.excl:
                for u, c in b.r.items():
                    if u != eng:
                        add((u, c), False)
        for b in writes:
            add(b.w, True)
            for u, c in b.r.items():
                add((u, c), pe)
        return deps

    def _emit_waits(self, eng, deps):
        seen = self.seen[eng]
        for u, c in deps.items():
            if seen.get(u, 0) >= c:
                continue
            seen[u] = c
            sem = self.sems[u]
            val = c * self.mult[u]
            self.prog[eng].append(lambda h, sem=sem, val=val: h.wait_ge(sem, val))
            self.n_wait += 1

    def _mark(self, unit, reads, writes):
        c = self.cnt[unit]
        for b in writes:
            b.w = (unit, c)
            b.r = {}
        for b in reads:
            if b.r.get(unit, 0) < c:
                b.r[unit] = c

    @staticmethod
    def _flat(bufs):
        out = []
        for b in bufs:
            if isinstance(b, (list, tuple)):
                out.extend(Emitter._flat(b))
            else:
                out.append(b)
        return out

    def op(self, eng, fn, reads=(), writes=(), signal=True):
        if self._chk():
            return
        reads, writes = self._flat(reads), self._flat(writes)
        deps = self._deps(eng, reads, writes)
        self._emit_waits(eng, deps)
        self.raw[eng] += 1
        self.cnt[eng] += 1
        if signal:
            sem = self.sems[eng]
            self.prog[eng].append(lambda h, fn=fn, sem=sem: fn(h).then_inc(sem, 1))
            self._mark(eng, reads, writes)
        else:
            self.prog[eng].append(lambda h, fn=fn: fn(h))
            self._mark(eng, reads, writes)
            self.cnt[eng] -= 1
        self.n_inst += 1

    def dma(self, out, in_, reads=(), writes=(), queue="sp"):
        if self._chk():
            return
        reads, writes = self._flat(reads), self._flat(writes)
        chan = self.chans[self.chan_rr % len(self.chans)]
        self.chan_rr += 1
        deps = self._deps(chan, reads, writes)
        if self.cnt[chan] > 0:
            deps[chan] = max(deps.get(chan, 0), self.cnt[chan])
        self._emit_waits(queue, deps)
        self.cnt[chan] += 1
        sem = self.sems[chan]
        self.prog[queue].append(lambda h, out=out, in_=in_, sem=sem: h.dma_start(out=out, in_=in_).then_inc(sem, 16))
        self._mark(chan, reads, writes)
        self.n_inst += 1

    def barrier_all(self):
        for e in self.ENGS:
            deps = {u: c for u, c in self.cnt.items() if c > 0 and u != e}
            if e != "pe" and self.cnt[e] > 0:
                deps[e] = self.cnt[e]
            self._emit_waits(e, deps)

    def finish(self):
        self.barrier_all()
        nc = self.nc
        prog = self.prog
        with nc.Block() as block:
            @block.tensor
            def _(h):
                for t in prog["pe"]:
                    t(h)

            @block.scalar
            def _(h):
                for t in prog["act"]:
                    t(h)

            @block.vector
            def _(h):
                for t in prog["dve"]:
                    t(h)

            @block.gpsimd
            def _(h):
                for t in prog["pool"]:
                    t(h)

            @block.sync
            def _(h):
                for t in prog["sp"]:
                    t(h)
        for cm in reversed(self._ctx):
            cm.__exit__(None, None, None)


VEC_SPEC = [("c", 8), ("cctx", 8), ("ng0", 8), ("ng1", 8), ("bm0", 24), ("bm1", 24),
            ("ba2_0", 4), ("ba2_1", 4), ("glag", 2), ("cw0", 8), ("cw1", 8), ("cw2", 8)]
for _d in range(2):
    for _j in range(4):
        VEC_SPEC.append(("ocw%d%d" % (_d, _j), 16))
    VEC_SPEC += [("ocb%d" % _d, 16), ("oba%d" % _d, 16), ("obx%d" % _d, 16), ("lam%d" % _d, 16)]
VEC_SPEC.append(("fg", 8))
VEC_OFF = {}
_o = 0
for _n, _w in VEC_SPEC:
    VEC_OFF[_n] = (_o, _w)
    _o += _w
NVEC = _o


def pack_vecs(b, inp):
    def fm(v):
        v = np.asarray(v, np.float32).reshape(-1)
        return v.reshape(v.size // 128, 128).T

    parts = {"c": inp["c"][b], "cctx": inp["c_ctx"], "ng0": inp["norm_g"][0], "ng1": inp["norm_g"][1],
             "bm0": inp["b_mod"][0], "bm1": inp["b_mod"][1], "ba2_0": inp["e_b_a2"][0, 0],
             "ba2_1": inp["e_b_a2"][0, 1], "glag": inp["e_gla_g"][0], "cw0": inp["e_conv_w"][0, 0],
             "cw1": inp["e_conv_w"][0, 1], "cw2": inp["e_conv_w"][0, 2], "fg": inp["final_g"]}
    for d in range(2):
        for j in range(4):
            parts["ocw%d%d" % (d, j)] = inp["o_conv_w"][0, d, j]
        parts["ocb%d" % d] = inp["o_conv_b"][0, d]
        parts["oba%d" % d] = inp["o_b_a"][0, d]
        parts["obx%d" % d] = inp["o_b_x"][0, d]
        parts["lam%d" % d] = inp["o_lam"][0, d]
    out = np.zeros((128, NVEC), np.float32)
    for n, w in VEC_SPEC:
        o, _ = VEC_OFF[n]
        out[:, o:o + w] = fm(parts[n])
    return out


def build(L, CTX, NT=256, dbg=False, npass=7, max_inst=None):
    assert CTX % NT == 0 and L % NT == 0 and NT % 128 == 0 and NT <= 512
    TOK = CTX + L
    NJ = NT // 128
    nc = bass.Bass("TRN2", target_bir_lowering=False)
    dt = nc.dram_tensor
    xin = dt("xin", [TOK, D], F32, kind="ExternalInput").ap()
    vecs_d = dt("vecs", [128, NVEC], F32, kind="ExternalInput").ap()
    w_mod = dt("w_mod", [2, D, 3 * D], F32, kind="ExternalInput").ap()
    e_w_in = dt("e_w_in", [D, EVEN_IN], F32, kind="ExternalInput").ap()
    e_w_a2 = dt("e_w_a2", [2, 16, 512], F32, kind="ExternalInput").ap()
    e_w_out = dt("e_w_out", [2 * D, D], F32, kind="ExternalInput").ap()
    o_w_in = dt("o_w_in", [D, 4 * D], F32, kind="ExternalInput").ap()
    o_w_a = dt("o_w_a", [2, 16, 128, 128], F32, kind="ExternalInput").ap()
    o_w_x = dt("o_w_x", [2, 16, 128, 128], F32, kind="ExternalInput").ap()
    o_w_out = dt("o_w_out", [2 * D, D], F32, kind="ExternalInput").ap()
    out_d = dt("out", [L, D], F32, kind="ExternalOutput").ap()
    sk = "ExternalOutput" if dbg else "Internal"
    ob_s = dt("ob_s", [8, 128, TOK], F32, kind=sk).ap()
    in0_s = dt("in0_s", [16, 128, TOK], BF16, kind=sk).ap()
    x1_s = dt("x1_s", [TOK, D], F32, kind=sk).ap()
    hb_s = dt("hb_s", [16, 128, TOK], F32, kind=sk).ap()
    in1_s = dt("in1_s", [16, 128, TOK], BF16, kind=sk).ap()

    em = Emitter(nc)
    em.max_inst = max_inst
    em.marks = []
    gs = ExitStack()

    uid = [0]

    def mk(stack, name, shape, dtype=F32, psum=False):
        uid[0] += 1
        name = "%s_%d" % (name, uid[0])
        t = stack.enter_context((nc.psum_tensor if psum else nc.sbuf_tensor)(name, shape, dtype))
        return t, Buf(name)

    banks = [mk(gs, "bank%d" % i, [128, 512], F32, psum=True) for i in range(8)]
    for _t, _b in banks:
        _b.excl = True

    ident, b_ident = mk(gs, "ident", [128, 128])
    maskF, b_maskF = mk(gs, "maskF", [128, 128])
    maskB, b_maskB = mk(gs, "maskB", [128, 128])
    ones_f, b_ones_f = mk(gs, "ones_f", [128, 128])
    ones_b, b_ones_b = mk(gs, "ones_b", [128, 128], BF16)
    ident_b, b_ident_b = mk(gs, "ident_b", [128, 128], BF16)
    rmask, b_rmask = mk(gs, "rmask", [128, 520])
    vecs, b_vecs = mk(gs, "vecs_sb", [128, NVEC])
    dv, b_dv = mk(gs, "dv", [128, 256])
    modT, b_modT = mk(gs, "modT", [128, 2, 48])
    stage = [mk(gs, "stage%d" % i, [128, 4096]) for i in range(2)]

    def V(name, lo=0, n=None):
        o, w = VEC_OFF[name]
        if n is None:
            n = w - lo
        return vecs[:, o + lo:o + lo + n]

    DV = {}
    _p = [0]

    def dvalloc(name, n):
        DV[name] = (_p[0], n)
        _p[0] += n

    for nm, n in [("sv", 16), ("A00", 8), ("A01", 8), ("A10", 8), ("A11", 8), ("nb0", 4), ("nb1", 4),
                  ("c1_0", 16), ("c1_1", 16), ("c2_0", 16), ("c2_1", 16), ("tmp", 16)]:
        dvalloc(nm, n)

    def DVv(name, lo=0, n=None):
        o, w = DV[name]
        if n is None:
            n = w - lo
        return dv[:, o + lo:o + lo + n]

    em.dma(vecs[:], vecs_d, writes=[b_vecs])
    em.op("pool", lambda h: h.memset(ident[:], 1.0), writes=[b_ident])
    em.op("pool", lambda h: h.affine_select(out=ident[:], in_=ident[:], pattern=[[-1, 128]], compare_op=ALU.is_equal,
                                            fill=0.0, base=0, channel_multiplier=1), reads=[b_ident], writes=[b_ident])
    em.op("pool", lambda h: h.memset(maskF[:], 1.0), writes=[b_maskF])
    em.op("pool", lambda h: h.affine_select(out=maskF[:], in_=maskF[:], pattern=[[1, 128]], compare_op=ALU.is_ge,
                                            fill=0.0, base=0, channel_multiplier=-1), reads=[b_maskF], writes=[b_maskF])
    em.op("pool", lambda h: h.memset(maskB[:], 1.0), writes=[b_maskB])
    em.op("pool", lambda h: h.affine_select(out=maskB[:], in_=maskB[:], pattern=[[-1, 128]], compare_op=ALU.is_ge,
                                            fill=0.0, base=0, channel_multiplier=1), reads=[b_maskB], writes=[b_maskB])
    em.op("pool", lambda h: h.memset(ones_f[:], 1.0), writes=[b_ones_f])
    em.op("pool", lambda h: h.memset(ones_b[:], 1.0), writes=[b_ones_b])
    em.op("dve", lambda h: h.tensor_copy(out=ident_b[:], in_=ident[:]), reads=[b_ident], writes=[b_ident_b])
    em.op("pool", lambda h: h.memset(rmask[:], 1.0), writes=[b_rmask])
    for q in range(5):
        em.op("pool", lambda h, q=q: h.memset(rmask[:, q * 128:q * 128 + 1], 0.0), reads=[b_rmask], writes=[b_rmask])

    sv = DVv("sv")
    sv3 = sv.rearrange("p (k w) -> p k w", w=2)
    em.op("act", lambda h: h.activation(out=sv3[:, :, 0], in_=V("c"), func=AF.Silu), reads=[b_vecs], writes=[b_dv])
    em.op("act", lambda h: h.activation(out=sv3[:, :, 1], in_=V("cctx"), func=AF.Silu), reads=[b_vecs, b_dv], writes=[b_dv])
    for d in range(2):
        em.op("dve", lambda h, d=d: h.tensor_scalar(out=DVv("nb%d" % d), in0=V("ba2_%d" % d), scalar1=-1.0, scalar2=None,
                                                    op0=ALU.mult), reads=[b_vecs, b_dv], writes=[b_dv])
    for d in range(2):
        em.op("act", lambda h, d=d: h.activation(out=DVv("tmp"), in_=V("lam%d" % d), func=AF.Exp, scale=-1.0),
              reads=[b_vecs, b_dv], writes=[b_dv])
        em.op("act", lambda h, d=d: h.activation(out=DVv("tmp"), in_=DVv("tmp"), func=AF.Ln, bias=1.0),
              reads=[b_dv], writes=[b_dv])
        em.op("dve", lambda h, d=d: h.tensor_scalar(out=DVv("c1_%d" % d), in0=DVv("tmp"), scalar1=-8.0, scalar2=None,
                                                    op0=ALU.mult), reads=[b_dv], writes=[b_dv])
        em.op("dve", lambda h, d=d: h.tensor_scalar(out=DVv("c2_%d" % d), in0=DVv("tmp"), scalar1=-16.0, scalar2=None,
                                                    op0=ALU.mult), reads=[b_dv], writes=[b_dv])

    pm, b_pm = banks[7]
    for li in range(2):
        for cb in range(6):
            st, b_st = stage[cb % 2]
            st3 = st.rearrange("p (k n) -> p k n", n=512)
            for kc in range(KC):
                em.dma(st3[:, kc, :], w_mod[li, kc * 128:(kc + 1) * 128, cb * 512:(cb + 1) * 512], writes=[b_st])
            for mm in range(4):
                m = cb * 4 + mm
                for kc in range(KC):
                    em.op("pe", lambda h, m=m, mm=mm, kc=kc, st3=st3: h.matmul(
                        pm[:, m * 2:m * 2 + 2], lhsT=st3[:, kc, mm * 128:(mm + 1) * 128], rhs=sv3[:, kc, :],
                        start=(kc == 0), stop=(kc == KC - 1)), reads=[b_st, b_dv], writes=[b_pm])
        pm3 = pm[:, 0:48].rearrange("p (m w) -> p m w", w=2)
        mo3 = modT[:, li, :].rearrange("p (m w) -> p m w", w=2)
        for w_ in range(2):
            em.op("dve", lambda h, w_=w_, li=li, pm3=pm3, mo3=mo3: h.tensor_tensor(
                out=mo3[:, :, w_], in0=pm3[:, :, w_], in1=V("bm%d" % li), op=ALU.add),
                reads=[b_pm, b_vecs], writes=[b_modT])
        for w_ in range(2):
            em.op("dve", lambda h, w_=w_, li=li, mo3=mo3: h.scalar_tensor_tensor(
                out=DVv("A%d%d" % (li, w_)), in0=mo3[:, 8:16, w_], scalar=1.0, in1=V("ng%d" % li),
                op0=ALU.add, op1=ALU.mult), reads=[b_modT, b_vecs, b_dv], writes=[b_dv])

    def mod_vec(li, which, part):
        mo3 = modT[:, li, :].rearrange("p (m w) -> p m w", w=2)
        return mo3[:, part * 8:(part + 1) * 8, which]

    eng_rr = [0]

    def cast_eng():
        e = ("dve", "act", "pool")[eng_rr[0] % 3]
        eng_rr[0] += 1
        return e

    def copy_op(eng, out, in_, reads, writes):
        if eng == "act":
            em.op("act", lambda h: h.activation(out=out, in_=in_, func=AF.Copy), reads=reads, writes=writes)
        else:
            em.op(eng, lambda h: h.tensor_copy(out=out, in_=in_), reads=reads, writes=writes)

    ld_rr = [0]

    def load_cast(dst, b_dst, src, ncols):
        c0 = 0
        while c0 < ncols:
            n = min(4096, ncols - c0)
            st, b_st = stage[ld_rr[0] % 2]
            ld_rr[0] += 1
            em.dma(st[:, 0:n], src[:, c0:c0 + n], writes=[b_st])
            if isinstance(b_dst, list):
                nb_ = Buf("wpart")
                b_dst.append(nb_)
                copy_op(cast_eng(), dst[:, c0:c0 + n], st[:, 0:n], [b_st], [nb_])
            else:
                copy_op(cast_eng(), dst[:, c0:c0 + n], st[:, 0:n], [b_st], [b_dst])
            c0 += n

    def bcast_row(stack, name, vec_ap, b_src):
        t, b_t = mk(stack, name, [128, 1024])
        dg, b_dg = mk(stack, name + "_dg", [128, 128])
        for kc in range(KC):
            bk, b_bk = banks[6 + (kc // 4)]
            em.op("dve", lambda h, kc=kc: h.tensor_scalar(out=dg[:], in0=ident[:], scalar1=vec_ap[:, kc:kc + 1], scalar2=None,
                                                          op0=ALU.mult), reads=[b_ident, b_src], writes=[b_dg])
            em.op("pe", lambda h, kc=kc, bk=bk: h.matmul(bk[:, (kc % 4) * 128:(kc % 4 + 1) * 128], lhsT=ones_f[:], rhs=dg[:],
                                                         start=True, stop=True), reads=[b_ones_f, b_dg], writes=[b_bk])
            em.op("act", lambda h, kc=kc, bk=bk: h.activation(out=t[:, kc * 128:(kc + 1) * 128],
                                                              in_=bk[:, (kc % 4) * 128:(kc % 4 + 1) * 128], func=AF.Copy),
                  reads=[b_bk], writes=[b_t])
        return t, b_t

    def supertiles(reverse):
        ctx_t = [(t0, True) for t0 in range(0, CTX, NT)]
        lat_t = [(t0, False) for t0 in range(CTX, TOK, NT)]
        if reverse:
            ctx_t, lat_t = ctx_t[::-1], lat_t[::-1]
        res = []
        for seq in (ctx_t, lat_t):
            for i, (t0, c) in enumerate(seq):
                res.append((t0, c, i == 0))
        return res

    class Front:
        def __init__(self, stack, xsrc, li):
            self.xsrc, self.li = xsrc, li
            self.xt = [mk(stack, "f_xt%d" % i, [128, D]) for i in range(2)]
            self.xn = [mk(stack, "f_xn%d" % i, [128, D], BF16) for i in range(2)]
            self.junk = mk(stack, "f_junk", [128, D], BF16)
            self.ss = [mk(stack, "f_ss%d" % i, [128, 1]) for i in range(2)]
            self.hT = [mk(stack, "f_hT%d" % i, [128, KC, NT], BF16) for i in range(2)]
            self.hT = [(t, (Buf("hTa"), Buf("hTd"))) for (t, _) in self.hT]
            self.n = 0
            self.k = 0

        def run(self, tok0, is_ctx):
            self.load(tok0)
            return self.finish(tok0, is_ctx)

        def load(self, tok0):
            assert NJ == 2
            self.pending = []
            for j in range(NJ):
                xt, b_xt = self.xt[j]
                xn, b_xn = self.xn[j]
                ss, b_ss = self.ss[j]
                junk, b_junk = self.junk
                self.pending.append((xn, b_xn))
                r0 = tok0 + j * 128
                em.dma(xt[:], self.xsrc[r0:r0 + 128, :], writes=[b_xt])
                em.op("act", lambda h, xt=xt, ss=ss: h.activation(out=junk[:], in_=xt[:], func=AF.Square, accum_out=ss[:]),
                      reads=[b_xt], writes=[b_junk, b_ss])
                em.op("dve", lambda h, ss=ss: h.tensor_scalar(out=ss[:], in0=ss[:], scalar1=1.0 / D, scalar2=EPS,
                                                              op0=ALU.mult, op1=ALU.add), reads=[b_ss], writes=[b_ss])
                em.op("act", lambda h, ss=ss: h.activation(out=ss[:], in_=ss[:], func=AF.Ln), reads=[b_ss], writes=[b_ss])
                em.op("act", lambda h, ss=ss: h.activation(out=ss[:], in_=ss[:], func=AF.Exp, scale=-0.5), reads=[b_ss], writes=[b_ss])
                em.op("pool", lambda h, xt=xt, xn=xn, ss=ss: h.tensor_scalar(out=xn[:], in0=xt[:], scalar1=ss[:, 0:1], scalar2=0.0,
                                                                             op0=ALU.mult, op1=ALU.add),
                      reads=[b_xt, b_ss], writes=[b_xn])

        def finish(self, tok0, is_ctx):
            hT, b_hT = self.hT[self.n % 2]
            self.n += 1
            w_ = 1 if is_ctx else 0
            A = DVv("A%d%d" % (self.li, w_))
            sh = mod_vec(self.li, w_, 0)
            for j in range(NJ):
                xn, b_xn = self.pending[j]
                bk, b_bk = banks[j % 2]
                bkb = bk[:, :].bitcast(BF16)
                for kc in range(KC):
                    em.op("pe", lambda h, xn=xn, kc=kc, bkb=bkb: h.transpose(
                        out=bkb[:, kc * 128:(kc + 1) * 128], in_=xn[:, kc * 128:(kc + 1) * 128], identity=ident_b[:]),
                        reads=[b_xn, b_ident_b], writes=[b_bk], signal=(kc == KC - 1))
                for kc in range(KC):
                    eng = "act" if j % 2 == 0 else "dve"
                    if eng == "act":
                        em.op("act", lambda h, kc=kc, bkb=bkb, j=j: h.activation(
                            out=hT[:, kc, j * 128:(j + 1) * 128], in_=bkb[:, kc * 128:(kc + 1) * 128], func=AF.Identity,
                            scale=A[:, kc:kc + 1], bias=sh[:, kc:kc + 1]), reads=[b_bk, b_dv, b_modT], writes=[b_hT[0]])
                    else:
                        em.op("dve", lambda h, kc=kc, bkb=bkb, j=j: h.tensor_scalar(
                            out=hT[:, kc, j * 128:(j + 1) * 128], in0=bkb[:, kc * 128:(kc + 1) * 128],
                            scalar1=A[:, kc:kc + 1], scalar2=sh[:, kc:kc + 1], op0=ALU.mult, op1=ALU.add),
                            reads=[b_bk, b_dv, b_modT], writes=[b_hT[1]])
            return hT, b_hT

    def proj_fm(bank_ap, b_bank, W, col0, hT, b_hT, b_W, M=128):
        for kc in range(KC):
            em.op("pe", lambda h, kc=kc: h.matmul(bank_ap, lhsT=W[:, kc, col0:col0 + M], rhs=hT[:, kc, :],
                                                  start=(kc == 0), stop=(kc == KC - 1)),
                  reads=[b_W, b_hT], writes=[b_bank], signal=(kc == KC - 1))

    def pass_gla(d):
        with ExitStack() as ps:
            fwd = (d == 0)
            ncol = 3072 if fwd else 2048
            W, b_W = mk(ps, "g_W", [128, KC, ncol], BF16)
            b_W = []
            Wa, b_Wa = mk(ps, "g_Wa", [128, KC, 16], BF16)
            b_Wa = []
            wa2, b_wa2 = mk(ps, "g_wa2", [16, 512])
            for kc in range(KC):
                load_cast(W[:, kc, :], b_W, e_w_in[kc * 128:(kc + 1) * 128, 0:ncol], ncol)
                ca = C_AF if fwd else C_AB
                load_cast(Wa[:, kc, :], b_Wa, e_w_in[kc * 128:(kc + 1) * 128, ca:ca + 16], 16)
            em.dma(wa2[:], e_w_a2[d], writes=[b_wa2])
            fr = Front(ps, xin, 0)
            v_sb, _ = mk(ps, "g_v", [128, NJ, 1024], BF16)
            b_v = (Buf("v_h0"), Buf("v_h1"))
            a_sb, b_a = mk(ps, "g_a", [16, NT])
            hb = []
            for i in range(4):
                hb.append(dict(sp=mk(ps, "g_sp%d" % i, [128, NT]), c=mk(ps, "g_c%d" % i, [128, NT]),
                               E1=mk(ps, "g_E1%d" % i, [128, NT]), E2=mk(ps, "g_E2%d" % i, [128, NT]),
                               qi=mk(ps, "g_qi%d" % i, [128, NT], BF16), ki=mk(ps, "g_ki%d" % i, [128, NT], BF16)))
            ktok = [mk(ps, "g_kt%d" % i, [128, 128], BF16) for i in range(4)]
            scT = [mk(ps, "g_sc%d" % i, [128, 128], BF16) for i in range(4)]
            S = [mk(ps, "g_S%d" % h_, [128, 256]) for h_ in range(4)]
            Sb = [mk(ps, "g_Sb%d" % h_, [128, 256], BF16) for h_ in range(4)]
            tmpS = [mk(ps, "g_tS%d" % i, [128, 256]) for i in range(4)]
            o_sb, b_o = mk(ps, "g_o", [128, 8, NT])
            if fwd:
                obl, b_obl = mk(ps, "g_obl", [128, 8, NT])
                sga, b_sga = mk(ps, "g_sga", [128, 8, NT])
                sq, b_sq = mk(ps, "g_sq", [128, 8, NT], BF16)
                rstd, b_rstd = mk(ps, "g_rstd", [128, 4, NT])
                og, b_og = mk(ps, "g_og", [128, 8, NT], BF16)
            for h_ in range(4):
                em.op("pool", lambda h, h_=h_: h.memset(S[h_][0][:], 0.0), writes=[S[h_][1]])
                em.op("pool", lambda h, h_=h_: h.memset(Sb[h_][0][:], 0.0), writes=[Sb[h_][1]])
            assert NT == 256
            PQK, PZ, PKT, PSI = banks[2], banks[3], banks[6], banks[7]
            POs = (banks[4], banks[5])
            b_sc, b_kt, b_inc = PSI[1], PKT[1], PSI[1]
            ps_kt = PKT[0][:, 0:64].bitcast(BF16)
            ps_sc = PSI[0][:, 0:128]
            ps_inc = PSI[0][:, 256:512]
            step_n = 0
            mask, b_mask = (maskF, b_maskF) if fwd else (maskB, b_maskB)
            cnt = 0
            sts = supertiles(reverse=not fwd)
            nxt = fr.run(sts[0][0], sts[0][1])
            for sidx, (tok0, is_ctx, first) in enumerate(sts):
                hT, b_hT = nxt
                em.mark("gla%d st%d v" % (d, tok0))
                v_banks = (PZ, PKT, POs[0], POs[1])
                for j in range(NJ):
                    for half in range(2):
                        pv, b_pv = v_banks[(j * 2 + half) % 4]
                        for kc in range(KC):
                            em.op("pe", lambda h, kc=kc, j=j, half=half, pv=pv, hT=hT: h.matmul(
                                pv[:, :], lhsT=hT[:, kc, j * 128:(j + 1) * 128],
                                rhs=W[:, kc, C_V + half * 512:C_V + (half + 1) * 512], start=(kc == 0), stop=(kc == KC - 1)),
                                reads=[b_W, b_hT], writes=[b_pv], signal=(kc == KC - 1))
                        copy_op("act" if half == 0 else "dve", v_sb[:, j, half * 512:(half + 1) * 512], pv[:, :], [b_pv], [b_v[half]])
                em.mark("gla%d st%d a" % (d, tok0))
                pa, b_pa = PZ
                proj_fm(pa[0:16, 0:NT], b_pa, Wa, 0, hT, b_hT, b_Wa, M=16)
                copy_op("act", a_sb[:], pa[0:16, 0:NT], [b_pa], [b_a])
                if fwd:
                    em.dma(obl[:], ob_s[:, :, tok0:tok0 + NT].rearrange("c p t -> p c t"), writes=[b_obl])
                nb = DVv("nb%d" % d)
                z_banks = (PZ, PKT)
                qk_banks = (PQK, POs[0], POs[1], PSI)
                zregs = []
                for h_ in range(4):
                    zb = z_banks[h_ // 2]
                    zreg = zb[0][:, (h_ % 2) * 256:(h_ % 2) * 256 + NT]
                    zregs.append((zreg, zb[1]))
                    em.op("pe", lambda h, h_=h_, zreg=zreg: h.matmul(zreg, lhsT=wa2[:, h_ * 128:(h_ + 1) * 128], rhs=a_sb[:],
                                                                    start=True, stop=True), reads=[b_wa2, b_a], writes=[zb[1]])
                for h_ in range(4):
                    B = hb[h_]
                    (sp, b_sp), (c_, b_c), (E1, b_E1), (E2, b_E2) = B["sp"], B["c"], B["E1"], B["E2"]
                    zreg, b_pz = zregs[h_]
                    em.op("act", lambda h, h_=h_, sp=sp, zreg=zreg: h.activation(out=sp[:], in_=zreg, func=AF.Exp, scale=-1.0,
                                                                                bias=nb[:, h_:h_ + 1]), reads=[b_pz, b_dv], writes=[b_sp])
                    em.op("act", lambda h, sp=sp: h.activation(out=sp[:], in_=sp[:], func=AF.Ln, bias=1.0), reads=[b_sp], writes=[b_sp])
                    if fwd:
                        em.op("dve", lambda h, sp=sp, c_=c_: h.tensor_tensor_scan(
                            out=c_[:], data0=rmask[:, 0:NT], data1=sp[:], initial=0.0, op0=ALU.mult, op1=ALU.add),
                            reads=[b_sp, b_rmask], writes=[b_c])
                    else:
                        em.op("dve", lambda h, sp=sp, c_=c_: h.tensor_tensor_scan(
                            out=c_[:, ::-1], data0=rmask[:, 1:NT + 1][:, ::-1], data1=sp[:, ::-1], initial=0.0,
                            op0=ALU.mult, op1=ALU.add), reads=[b_sp, b_rmask], writes=[b_c])
                    em.op("act", lambda h, c_=c_, E1=E1: h.activation(out=E1[:], in_=c_[:], func=AF.Exp, scale=-1.0 / 16), reads=[b_c], writes=[b_E1])
                    em.op("act", lambda h, c_=c_, E2=E2: h.activation(out=E2[:], in_=c_[:], func=AF.Exp, scale=1.0 / 16), reads=[b_c], writes=[b_E2])
                for h_ in range(4):
                    qb = qk_banks[h_]
                    proj_fm(qb[0][:, 0:NT], qb[1], W, C_Q + h_ * 128, hT, b_hT, b_W)
                    proj_fm(qb[0][:, 256:256 + NT], qb[1], W, C_K + h_ * 128, hT, b_hT, b_W)
                for h_ in range(4):
                    B = hb[h_]
                    (E1, b_E1), (E2, b_E2), (qi, b_qi), (ki, b_ki) = B["E1"], B["E2"], B["qi"], B["ki"]
                    qb = qk_banks[h_]
                    pq, pk, b_pq = qb[0][:, 0:NT], qb[0][:, 256:256 + NT], qb[1]
                    em.op("dve", lambda h, qi=qi, E1=E1, pq=pq: h.scalar_tensor_tensor(
                        out=qi[:], in0=pq, scalar=128.0 ** -0.5, in1=E1[:], op0=ALU.mult, op1=ALU.mult),
                        reads=[b_pq, b_E1], writes=[b_qi])
                    em.op("dve", lambda h, ki=ki, E2=E2, pk=pk: h.tensor_tensor(out=ki[:], in0=pk, in1=E2[:], op=ALU.mult),
                          reads=[b_pq, b_E2], writes=[b_ki])
                if sidx + 1 < len(sts):
                    fr.load(sts[sidx + 1][0])
                js = range(NJ) if fwd else range(NJ - 1, -1, -1)
                for j in js:
                    for h_ in range(4):
                        B = hb[h_]
                        (E1, b_E1), (qi, b_qi), (ki, b_ki) = B["E1"], B["qi"], B["ki"]
                        tsl = slice(j * 128, (j + 1) * 128)
                        kt, b_ktok = ktok[h_]
                        sc, b_scT = scT[h_]
                        tS, b_tS = tmpS[h_]
                        PO = POs[step_n % 2]
                        step_n += 1
                        St, b_S = S[h_]
                        Sbt, b_Sb = Sb[h_]
                        em.op("pe", lambda h, ki=ki, tsl=tsl: h.transpose(out=ps_kt, in_=ki[:, tsl], identity=ident_b[:]),
                              reads=[b_ki, b_ident_b], writes=[b_kt])
                        copy_op("act", kt[:], ps_kt, [b_kt], [b_ktok])
                        em.op("pe", lambda h, ki=ki, qi=qi, tsl=tsl: h.matmul(ps_sc, lhsT=ki[:, tsl], rhs=qi[:, tsl],
                                                                             start=True, stop=True),
                              reads=[b_ki, b_qi], writes=[b_sc])
                        em.op("dve", lambda h, sc=sc: h.tensor_tensor(out=sc[:], in0=ps_sc, in1=mask[:], op=ALU.mult),
                              reads=[b_sc, b_mask], writes=[b_scT])
                        for c in range(2):
                            po, b_po = PO[0][:, c * 128:(c + 1) * 128], PO[1]
                            vs = slice(h_ * 256 + c * 128, h_ * 256 + (c + 1) * 128)
                            em.op("pe", lambda h, po=po, vs=vs, sc=sc, j=j: h.matmul(
                                po, lhsT=v_sb[:, j, vs], rhs=sc[:], start=True, stop=False),
                                reads=[b_v, b_scT], writes=[b_po], signal=False)
                            em.op("pe", lambda h, po=po, c=c, Sbt=Sbt, qi=qi, tsl=tsl: h.matmul(
                                po, lhsT=Sbt[:, c * 128:(c + 1) * 128], rhs=qi[:, tsl], start=False, stop=True),
                                reads=[b_Sb, b_qi], writes=[b_po])
                        em.op("pe", lambda h, kt=kt, j=j, h_=h_: h.matmul(ps_inc, lhsT=kt[:], rhs=v_sb[:, j, h_ * 256:(h_ + 1) * 256],
                                                                          start=True, stop=True), reads=[b_ktok, b_v], writes=[b_inc])
                        dcol = j * 128 + (127 if fwd else 0)
                        em.op("dve", lambda h, tS=tS, St=St: h.tensor_tensor(out=tS[:], in0=ps_inc, in1=St[:], op=ALU.add),
                              reads=[b_inc, b_S], writes=[b_tS])
                        em.op("act", lambda h, tS=tS, St=St, E1=E1, dcol=dcol: h.activation(out=St[:], in_=tS[:], func=AF.Copy,
                                                                                          scale=E1[:, dcol:dcol + 1]),
                              reads=[b_tS, b_E1], writes=[b_S])
                        em.op("pool", lambda h, tS=tS, Sbt=Sbt, E1=E1, dcol=dcol: h.tensor_scalar(
                            out=Sbt[:], in0=tS[:], scalar1=E1[:, dcol:dcol + 1], scalar2=0.0, op0=ALU.mult, op1=ALU.add),
                            reads=[b_tS, b_E1], writes=[b_Sb])
                        po3 = PO[0][:, 0:256].rearrange("p (c t) -> p c t", c=2)
                        b_po = PO[1]
                        if fwd:
                            em.op("dve", lambda h, po3=po3, h_=h_, tsl=tsl: h.tensor_tensor(
                                out=o_sb[:, 2 * h_:2 * h_ + 2, tsl], in0=po3, in1=obl[:, 2 * h_:2 * h_ + 2, tsl], op=ALU.add),
                                reads=[b_po, b_obl], writes=[b_o])
                        else:
                            copy_op("act", o_sb[:, 2 * h_:2 * h_ + 2, tsl], po3, [b_po], [b_o])
                if sidx + 1 < len(sts):
                    nxt = fr.finish(sts[sidx + 1][0], sts[sidx + 1][1])
                if not fwd:
                    em.dma(ob_s[:, :, tok0:tok0 + NT].rearrange("c p t -> p c t"), o_sb[:], reads=[b_o], writes=[Buf()])
                    continue
                for cc in range(8):
                    pg, b_pg = (PQK, PSI)[cc % 2]
                    proj_fm(pg[:, 0:NT], b_pg, W, C_GA + cc * 128, hT, b_hT, b_W)
                    em.op("act", lambda h, pg=pg, cc=cc: h.activation(out=sga[:, cc, :], in_=pg[:, 0:NT], func=AF.Silu),
                          reads=[b_pg], writes=[b_sga])
                em.op("act", lambda h: h.activation(out=sq[:], in_=o_sb[:], func=AF.Square), reads=[b_o], writes=[b_sq])
                n_banks = (PZ, PKT)
                for h_ in range(4):
                    nbk = n_banks[h_ // 2]
                    reg = nbk[0][:, (h_ % 2) * 256:(h_ % 2) * 256 + NT]
                    for c in range(2):
                        em.op("pe", lambda h, c=c, h_=h_, reg=reg: h.matmul(reg, lhsT=ones_b[:], rhs=sq[:, 2 * h_ + c, :], start=(c == 0), stop=(c == 1)),
                              reads=[b_ones_b, b_sq], writes=[nbk[1]], signal=(c == 1))
                for q_ in range(2):
                    nbk = n_banks[q_]
                    src = nbk[0][:, 0:512].rearrange("p (n t) -> p n t", n=2)[:, :, 0:NT]
                    em.op("dve", lambda h, q_=q_, src=src: h.tensor_scalar(out=rstd[:, 2 * q_:2 * q_ + 2, :], in0=src, scalar1=1.0 / 256, scalar2=EPS,
                                                                          op0=ALU.mult, op1=ALU.add), reads=[nbk[1]], writes=[b_rstd])
                em.op("act", lambda h: h.activation(out=rstd[:], in_=rstd[:], func=AF.Ln), reads=[b_rstd], writes=[b_rstd])
                em.op("act", lambda h: h.activation(out=rstd[:], in_=rstd[:], func=AF.Exp, scale=-0.5), reads=[b_rstd], writes=[b_rstd])
                for ch in range(8):
                    c = ch % 2
                    em.op("dve", lambda h, ch=ch, c=c: h.scalar_tensor_tensor(
                        out=o_sb[:, ch, :], in0=o_sb[:, ch, :], scalar=V("glag", c, 1), in1=sga[:, ch, :], op0=ALU.mult, op1=ALU.mult),
                        reads=[b_o, b_sga, b_vecs, b_sq], writes=[b_o])
                for ch in range(8):
                    eng = "dve" if ch % 2 == 0 else "pool"
                    em.op(eng, lambda h, ch=ch: h.tensor_tensor(out=og[:, ch, :], in0=o_sb[:, ch, :], in1=rstd[:, ch // 2, :], op=ALU.mult),
                          reads=[b_o, b_rstd], writes=[b_og])
                em.dma(in0_s[0:8, :, tok0:tok0 + NT].rearrange("c p t -> p c t"), og[:], reads=[b_og], writes=[Buf()])
            em.barrier_all()

    def pass_sconv():
        with ExitStack() as ps:
            W, b_W = mk(ps, "c_W", [128, KC, 4096], BF16)
            b_W = []
            for kc in range(KC):
                load_cast(W[:, kc, :], b_W, e_w_in[kc * 128:(kc + 1) * 128, C_CB:C_CB + 4096], 4096)
            em.mark("sconv weights done")
            fr = Front(ps, xin, 0)
            tcc = [mk(ps, "c_tcc%d" % i, [128, NT]) for i in range(2)]
            sgb = [mk(ps, "c_sgb%d" % i, [128, NT]) for i in range(2)]
            z = [mk(ps, "c_z%d" % i, [128, NT]) for i in range(2)]
            zc = [mk(ps, "c_zc%d" % i, [128, NT]) for i in range(2)]
            y_sb = [mk(ps, "c_y%d" % i, [128, 8, NT], BF16) for i in range(2)]
            n = 0
            sts = supertiles(False)
            nxt = fr.run(sts[0][0], sts[0][1])
            for si, (tok0, is_ctx, first) in enumerate(sts):
                hT, b_hT = nxt
                RW = NT if is_ctx else 64
                assert (not is_ctx) or CTX == NT
                y, b_y = y_sb[si % 2]
                for cc in range(8):
                    em.mark("sconv st%d cc%d" % (tok0, cc))
                    if cc == 1 and si + 1 < len(sts):
                        fr.load(sts[si + 1][0])
                    i2 = n % 2
                    n += 1
                    bA, b_bA = banks[2 + 4 * i2]
                    bB, b_bB = banks[3 + 4 * i2]
                    regs = [(bA[:, 0:NT], b_bA), (bA[:, 256:256 + NT], b_bA), (bB[:, 0:NT], b_bB), (bB[:, 256:256 + NT], b_bB)]
                    for qi_, col in enumerate((0, 1024, 2048, 3072)):
                        proj_fm(regs[qi_][0], regs[qi_][1], W, col + cc * 128, hT, b_hT, b_W)
                    (t_, b_t), (sg, b_sg), (z_, b_z), (zc_, b_zc) = tcc[i2], sgb[i2], z[i2], zc[i2]
                    em.op("act", lambda h, t_=t_, r=regs[1][0]: h.activation(out=t_[:], in_=r, func=AF.Copy), reads=[b_bA], writes=[b_t])
                    em.op("act", lambda h, sg=sg, r=regs[3][0]: h.activation(out=sg[:], in_=r, func=AF.Silu), reads=[b_bB], writes=[b_sg])
                    em.op("dve", lambda h, z_=z_, t_=t_, r=regs[2][0]: h.tensor_tensor(out=z_[:], in0=r, in1=t_[:], op=ALU.mult),
                          reads=[b_bB, b_t, b_sg], writes=[b_z])
                    z3 = z_.rearrange("p (r w) -> p r w", w=RW)
                    zc3 = zc_.rearrange("p (r w) -> p r w", w=RW)
                    em.op("dve", lambda h, z_=z_, zc_=zc_, cc=cc: h.tensor_scalar(out=zc_[:], in0=z_[:], scalar1=V("cw1", cc, 1), scalar2=0.0,
                                                                                  op0=ALU.mult, op1=ALU.add), reads=[b_z, b_vecs], writes=[b_zc])
                    em.op("dve", lambda h, z3=z3, zc3=zc3, cc=cc, RW=RW: h.scalar_tensor_tensor(
                        out=zc3[:, :, 1:RW], in0=z3[:, :, 0:RW - 1], scalar=V("cw0", cc, 1), in1=zc3[:, :, 1:RW], op0=ALU.mult, op1=ALU.add),
                        reads=[b_z, b_zc, b_vecs], writes=[b_zc])
                    em.op("dve", lambda h, z3=z3, zc3=zc3, cc=cc, RW=RW: h.scalar_tensor_tensor(
                        out=zc3[:, :, 0:RW - 1], in0=z3[:, :, 1:RW], scalar=V("cw2", cc, 1), in1=zc3[:, :, 0:RW - 1], op0=ALU.mult, op1=ALU.add),
                        reads=[b_z, b_zc, b_vecs], writes=[b_zc])
                    em.op("dve", lambda h, zc_=zc_, r=regs[0][0]: h.tensor_tensor(out=zc_[:], in0=r, in1=zc_[:], op=ALU.mult),
                          reads=[b_bA, b_zc], writes=[b_zc])
                    em.op("dve", lambda h, zc_=zc_, sg=sg, y=y, cc=cc: h.tensor_tensor(out=y[:, cc, :], in0=zc_[:], in1=sg[:], op=ALU.mult),
                          reads=[b_zc, b_sg], writes=[b_y])
                if si + 1 < len(sts):
                    nxt = fr.finish(sts[si + 1][0], sts[si + 1][1])
                em.mark("sconv st%d store" % tok0)
                em.dma(in0_s[8:16, :, tok0:tok0 + NT].rearrange("c p t -> p c t"), y[:], reads=[b_y], writes=[Buf()])
            em.barrier_all()

    def pass_out(li):
        with ExitStack() as ps:
            last = (li == 1)
            W, b_W = mk(ps, "o_W", [128, 16, 1024], BF16)
            b_W = []
            wsrc = o_w_out if last else e_w_out
            for kc in range(16):
                load_cast(W[:, kc, :], b_W, wsrc[kc * 128:(kc + 1) * 128, :], 1024)
            g_bc, b_gbc = bcast_row(ps, "o_gbc", mod_vec(li, 0, 2), b_modT)
            if last:
                f_bc, b_fbc = bcast_row(ps, "o_fbc", V("fg"), b_vecs)
            else:
                gc_bc, b_gcbc = bcast_row(ps, "o_gcbc", mod_vec(li, 1, 2), b_modT)
            inn = [mk(ps, "o_in%d" % i, [128, 16, NT], BF16) for i in range(2)]
            xt = [mk(ps, "o_xt%d" % i, [128, D]) for i in range(2 * NJ)]
            xo = [mk(ps, "o_xo%d" % i, [128, D]) for i in range(2)]
            tmp = [mk(ps, "o_tmp%d" % i, [128, 512]) for i in range(2)]
            ss = [mk(ps, "o_ss%d" % i, [128, 1]) for i in range(2)]
            junk, b_junk = mk(ps, "o_junk", [128, D], BF16)
            src_in = in1_s if last else in0_s
            src_x = x1_s if last else xin
            k = 0
            n = 0
            sts = [st for st in supertiles(False) if not (last and st[1])]

            def issue_loads(si):
                tok0 = sts[si][0]
                it, b_it = inn[si % 2]
                em.dma(it[:], src_in[:, :, tok0:tok0 + NT].rearrange("c p t -> p c t"), writes=[b_it])
                for j in range(NJ):
                    x_, b_x = xt[(si % 2) * NJ + j]
                    r0 = tok0 + j * 128
                    em.dma(x_[:], src_x[r0:r0 + 128, :], writes=[b_x])

            issue_loads(0)
            for si, (tok0, is_ctx, first) in enumerate(sts):
                if si + 1 < len(sts):
                    issue_loads(si + 1)
                it, b_it = inn[si % 2]
                gb_, b_gb_ = (gc_bc, b_gcbc) if (is_ctx and not last) else (g_bc, b_gbc)
                for j in range(NJ):
                    x_, b_x = xt[(si % 2) * NJ + j]
                    xo_, b_xo = xo[k % 2]
                    ss_, b_ss = ss[k % 2]
                    k += 1
                    r0 = tok0 + j * 128
                    for half in range(2):
                        bk, b_bk = banks[2 + n % 4]
                        t_, b_t = tmp[n % 2]
                        n += 1
                        hs = slice(half * 512, (half + 1) * 512)
                        for kc in range(16):
                            em.op("pe", lambda h, kc=kc, j=j, hs=hs, bk=bk, it=it: h.matmul(
                                bk[:, :], lhsT=it[:, kc, j * 128:(j + 1) * 128], rhs=W[:, kc, hs], start=(kc == 0), stop=(kc == 15)),
                                reads=[b_it, b_W], writes=[b_bk], signal=(kc == 15))
                        em.op("dve", lambda h, bk=bk, t_=t_, hs=hs, gb_=gb_: h.tensor_tensor(out=t_[:], in0=bk[:, :], in1=gb_[:, hs], op=ALU.mult),
                              reads=[b_bk, b_gb_], writes=[b_t])
                        em.op("pool", lambda h, t_=t_, x_=x_, xo_=xo_, hs=hs: h.tensor_tensor(out=xo_[:, hs], in0=t_[:], in1=x_[:, hs], op=ALU.add),
                              reads=[b_t, b_x], writes=[b_xo])
                    if not last:
                        em.dma(x1_s[r0:r0 + 128, :], xo_[:], reads=[b_xo], writes=[Buf()])
                    else:
                        em.op("act", lambda h, xo_=xo_, ss_=ss_: h.activation(out=junk[:], in_=xo_[:], func=AF.Square, accum_out=ss_[:]),
                              reads=[b_xo], writes=[b_junk, b_ss])
                        em.op("dve", lambda h, ss_=ss_: h.tensor_scalar(out=ss_[:], in0=ss_[:], scalar1=1.0 / D, scalar2=EPS,
                                                                        op0=ALU.mult, op1=ALU.add), reads=[b_ss], writes=[b_ss])
                        em.op("act", lambda h, ss_=ss_: h.activation(out=ss_[:], in_=ss_[:], func=AF.Ln), reads=[b_ss], writes=[b_ss])
                        em.op("act", lambda h, ss_=ss_: h.activation(out=ss_[:], in_=ss_[:], func=AF.Exp, scale=-0.5), reads=[b_ss], writes=[b_ss])
                        em.op("dve", lambda h, xo_=xo_, ss_=ss_: h.scalar_tensor_tensor(
                            out=xo_[:], in0=xo_[:], scalar=ss_[:, 0:1], in1=f_bc[:], op0=ALU.mult, op1=ALU.mult),
                            reads=[b_xo, b_ss, b_fbc], writes=[b_xo])
                        em.dma(out_d[r0 - CTX:r0 - CTX + 128, :], xo_[:], reads=[b_xo], writes=[Buf()])
            em.barrier_all()

    def pass_rglru(d):
        NB = 2
        NG = 16 // NB
        with ExitStack() as ps:
            fwd = (d == 0)
            ncol = 4096 if fwd else 2048
            W, b_W = mk(ps, "r_W", [128, KC, ncol], BF16)
            b_W = []
            wa, b_wa = mk(ps, "r_wa", [128, 16, 128], BF16)
            wx, b_wx = mk(ps, "r_wx", [128, 16, 128], BF16)
            for kc in range(KC):
                load_cast(W[:, kc, :], b_W, o_w_in[kc * 128:(kc + 1) * 128, 0:ncol], ncol)
            for (wt, b_wt, src) in ((wa, b_wa, o_w_a), (wx, b_wx, o_w_x)):
                st, b_st = stage[ld_rr[0] % 2]
                ld_rr[0] += 1
                em.dma(st[:, 0:2048].rearrange("p (n j) -> p n j", j=128), src[d].rearrange("n i j -> i n j"), writes=[b_st])
                copy_op(cast_eng(), wt[:].rearrange("p n j -> p (n j)"), st[:, 0:2048], [b_st], [b_wt])
            fr = Front(ps, x1_s, 1)
            H = 3
            xr, _ = mk(ps, "r_xr", [128, 16, NT + H])
            xr_bufs = [Buf("xr%d" % g) for g in range(NG)]
            carry, _ = mk(ps, "r_carry", [128, 16])
            carry_bufs = [Buf("carry%d" % g) for g in range(NG)]
            XC = [mk(ps, "r_xc%d" % i, [128, NB, NT]) for i in range(3)]
            XC = [(t, tuple(Buf("xc_b%d" % nb) for nb in range(NB))) for (t, _) in XC]
            XCB = [mk(ps, "r_xcb%d" % i, [128, NB, NT], BF16) for i in range(2)]
            RR = [mk(ps, "r_r%d" % i, [128, NB, NT]) for i in range(2)]
            II = [mk(ps, "r_i%d" % i, [128, NB, NT]) for i in range(2)]
            AA = [mk(ps, "r_a%d" % i, [128, NB, NT]) for i in range(2)]
            A2 = [mk(ps, "r_a2%d" % i, [128, NB, NT]) for i in range(2)]
            HH = [mk(ps, "r_h%d" % i, [128, NB, NT]) for i in range(2)]
            if fwd:
                HBL = [mk(ps, "r_hbl%d" % i, [128, NB, NT]) for i in range(2)]
                SG = [mk(ps, "r_sg%d" % i, [128, NB, NT]) for i in range(2)]
                y_sb = [mk(ps, "r_y%d" % i, [128, 16, NT], BF16) for i in range(2)]
            d0 = H if fwd else 0
            hl = 0 if fwd else NT
            em.op("pool", lambda h: h.memset(carry[:], 0.0), writes=carry_bufs)
            c1 = DVv("c1_%d" % d)
            c2 = DVv("c2_%d" % d)

            def reg_of(bank, nb):
                return bank[0][:, nb * 256:nb * 256 + NT], bank[1]

            items = []
            for si, (tok0, is_ctx, first) in enumerate(supertiles(reverse=not fwd)):
                stc = dict(si=si, tok0=tok0, is_ctx=is_ctx, first=first, need_out=not is_ctx)
                for g in range(NG):
                    items.append(dict(st=stc, g=g, k=len(items)))

            st_list = []
            for it_ in items:
                if it_["g"] == 0:
                    st_list.append(it_["st"])
            st_list[0]["hT"], st_list[0]["b_hT"] = fr.run(st_list[0]["tok0"], st_list[0]["is_ctx"])

            def stage_A(it):
                stc, g, k = it["st"], it["g"], it["k"]
                if g == 0:
                    if fwd and stc["need_out"]:
                        stc["y"], stc["b_y"] = y_sb[stc["si"] % 2]
                nsi = stc["si"] + 1
                if g == 1 and nsi < len(st_list):
                    fr.load(st_list[nsi]["tok0"])
                hT, b_hT = stc["hT"], stc["b_hT"]
                b_xg = xr_bufs[g]
                n0 = g * NB
                if stc["first"]:
                    em.op("pool", lambda h, n0=n0: h.memset(xr[:, n0:n0 + NB, hl:hl + H], 0.0), writes=[b_xg])
                bank = banks[2 + k % 2]
                for nb in range(NB):
                    reg, b_bk = reg_of(bank, nb)
                    proj_fm(reg, b_bk, W, (n0 + nb) * 128, hT, b_hT, b_W)
                if g == NG - 1 and nsi < len(st_list):
                    st_list[nsi]["hT"], st_list[nsi]["b_hT"] = fr.finish(st_list[nsi]["tok0"], st_list[nsi]["is_ctx"])

            def stage_Ae(it):
                g, k = it["g"], it["k"]
                b_xg = xr_bufs[g]
                n0 = g * NB
                bank = banks[2 + k % 2]
                src = bank[0][:, 0:NB * 256].rearrange("p (n t) -> p n t", n=NB)[:, :, 0:NT]
                copy_op("act", xr[:, n0:n0 + NB, d0:d0 + NT], src, [bank[1]], [b_xg])

            def stage_B(it):
                stc, g, k = it["st"], it["g"], it["k"]
                b_xg = xr_bufs[g]
                n0 = g * NB
                xc, b_xc = XC[k % 3]
                xcb, b_xcb = XCB[k % 2]
                for nb in range(NB):
                    n = n0 + nb
                    row = xr[:, n, :]
                    cw = lambda j, n=n: V("ocw%d%d" % (d, j), n, 1)
                    em.op("pool", lambda h, nb=nb, n=n, row=row, cw=cw, xc=xc: h.tensor_scalar(
                        out=xc[:, nb, :], in0=row[:, d0:d0 + NT], scalar1=cw(3), scalar2=V("ocb%d" % d, n, 1), op0=ALU.mult, op1=ALU.add),
                        reads=[b_xg, b_vecs], writes=[b_xc[nb]])
                for s_ in range(1, 4):
                    for nb in range(NB):
                        n = n0 + nb
                        row = xr[:, n, :]
                        cw = lambda j, n=n: V("ocw%d%d" % (d, j), n, 1)
                        off = d0 - s_ if fwd else d0 + s_
                        em.op("dve", lambda h, nb=nb, row=row, cw=cw, s_=s_, off=off, xc=xc: h.scalar_tensor_tensor(
                            out=xc[:, nb, :], in0=row[:, off:off + NT], scalar=cw(3 - s_), in1=xc[:, nb, :], op0=ALU.mult, op1=ALU.add),
                            reads=[b_xg, b_xc[nb], b_vecs], writes=[b_xc[nb]])
                src0 = (d0 + NT - H) if fwd else d0
                em.op("pool", lambda h, n0=n0, src0=src0: h.tensor_copy(out=xr[:, n0:n0 + NB, hl:hl + H], in_=xr[:, n0:n0 + NB, src0:src0 + H]),
                      reads=[b_xg, b_xc], writes=[b_xg])
                em.op("pool", lambda h, xc=xc, xcb=xcb: h.tensor_copy(out=xcb[:], in_=xc[:]), reads=[b_xc], writes=[b_xcb])

            def stage_C(it):
                stc, g, k = it["st"], it["g"], it["k"]
                n0 = g * NB
                xcb, b_xcb = XCB[k % 2]
                r_, b_r = RR[k % 2]
                i_, b_i = II[k % 2]
                bank_r = banks[4 + k % 2]
                bank_i = banks[6 + k % 2]
                for nb in range(NB):
                    n = n0 + nb
                    reg, b_bk = reg_of(bank_r, nb)
                    em.op("pe", lambda h, reg=reg, n=n, nb=nb, xcb=xcb: h.matmul(reg, lhsT=wa[:, n, :], rhs=xcb[:, nb, :], start=True, stop=True),
                          reads=[b_wa, b_xcb], writes=[b_bk])
                for nb in range(NB):
                    n = n0 + nb
                    reg2, b_bk2 = reg_of(bank_i, nb)
                    em.op("pe", lambda h, reg2=reg2, n=n, nb=nb, xcb=xcb: h.matmul(reg2, lhsT=wx[:, n, :], rhs=xcb[:, nb, :], start=True, stop=True),
                          reads=[b_wx, b_xcb], writes=[b_bk2])
                for nb in range(NB):
                    n = n0 + nb
                    reg, b_bk = reg_of(bank_r, nb)
                    em.op("act", lambda h, reg=reg, n=n, nb=nb, r_=r_: h.activation(out=r_[:, nb, :], in_=reg, func=AF.Sigmoid, bias=V("oba%d" % d, n, 1)),
                          reads=[b_bk, b_vecs], writes=[b_r])
                for nb in range(NB):
                    n = n0 + nb
                    reg2, b_bk2 = reg_of(bank_i, nb)
                    em.op("act", lambda h, reg2=reg2, n=n, nb=nb, i_=i_: h.activation(out=i_[:, nb, :], in_=reg2, func=AF.Sigmoid, bias=V("obx%d" % d, n, 1)),
                          reads=[b_bk2, b_vecs], writes=[b_i])
                if fwd and stc["need_out"]:
                    hbl, b_hbl = HBL[k % 2]
                    em.dma(hbl[:], hb_s[n0:n0 + NB, :, stc["tok0"]:stc["tok0"] + NT].rearrange("c p t -> p c t"), writes=[b_hbl])
                    bank = banks[2 + (k + 1) % 2]
                    for nb in range(NB):
                        reg, b_bk = reg_of(bank, nb)
                        proj_fm(reg, b_bk, W, 2048 + (n0 + nb) * 128, stc["hT"], stc["b_hT"], b_W)

            def stage_D(it):
                stc, g, k = it["st"], it["g"], it["k"]
                n0 = g * NB
                xc, b_xc = XC[k % 3]
                r_, b_r = RR[k % 2]
                i_, b_i = II[k % 2]
                a_, b_a = AA[k % 2]
                a2, b_a2 = A2[k % 2]
                hh, b_hh = HH[k % 2]
                b_cg = carry_bufs[g]
                out_fwd = fwd and stc["need_out"]
                if out_fwd:
                    sg, b_sg = SG[k % 2]
                    gbank = banks[2 + (k + 1) % 2]
                    gsrc = gbank[0][:, 0:NB * 256].rearrange("p (n t) -> p n t", n=NB)[:, :, 0:NT]
                    em.op("act", lambda h, sg=sg, gsrc=gsrc: h.activation(out=sg[:], in_=gsrc, func=AF.Sigmoid), reads=[gbank[1]], writes=[b_sg])
                    em.op("dve", lambda h, sg=sg, gsrc=gsrc: h.tensor_tensor(out=sg[:], in0=gsrc, in1=sg[:], op=ALU.mult), reads=[gbank[1], b_sg], writes=[b_sg])
                for nb in range(NB):
                    n = n0 + nb
                    em.op("act", lambda h, n=n, nb=nb, a_=a_, r_=r_: h.activation(out=a_[:, nb, :], in_=r_[:, nb, :], func=AF.Exp, scale=c1[:, n:n + 1]),
                          reads=[b_r, b_dv], writes=[b_a])
                em.op("pool", lambda h, a2=a2, a_=a_: h.tensor_tensor(out=a2[:], in0=a_[:], in1=a_[:], op=ALU.mult), reads=[b_a], writes=[b_a2])
                em.op("act", lambda h, a2=a2: h.activation(out=a2[:], in_=a2[:], func=AF.Ln, scale=-1.0, bias=1.0), reads=[b_a2], writes=[b_a2])
                em.op("act", lambda h, a2=a2: h.activation(out=a2[:], in_=a2[:], func=AF.Exp, scale=0.5), reads=[b_a2], writes=[b_a2])
                em.op("pool", lambda h, i_=i_, xc=xc: h.tensor_tensor(out=i_[:], in0=i_[:], in1=xc[:], op=ALU.mult), reads=[b_i, b_xc], writes=[b_i])
                em.op("dve", lambda h, i_=i_, a2=a2: h.tensor_tensor(out=i_[:], in0=i_[:], in1=a2[:], op=ALU.mult), reads=[b_i, b_a2], writes=[b_i])
                for nb in range(NB):
                    n = n0 + nb
                    if fwd:
                        em.op("dve", lambda h, n=n, nb=nb, hh=hh, a_=a_, i_=i_: h.tensor_tensor_scan(
                            out=hh[:, nb, :], data0=a_[:, nb, :], data1=i_[:, nb, :], initial=carry[:, n:n + 1], op0=ALU.mult, op1=ALU.add),
                            reads=[b_a, b_i, b_cg], writes=[b_hh])
                    else:
                        em.op("dve", lambda h, n=n, nb=nb, hh=hh, a_=a_, i_=i_: h.tensor_tensor_scan(
                            out=hh[:, nb, ::-1], data0=a_[:, nb, ::-1], data1=i_[:, nb, ::-1], initial=carry[:, n:n + 1],
                            op0=ALU.mult, op1=ALU.add), reads=[b_a, b_i, b_cg], writes=[b_hh])
                lc = NT - 1 if fwd else 0
                em.op("pool", lambda h, n0=n0, lc=lc, hh=hh: h.tensor_copy(out=carry[:, n0:n0 + NB], in_=hh[:, :, lc]), reads=[b_hh], writes=[b_cg])
                if not stc["need_out"]:
                    return
                tok0 = stc["tok0"]
                if not fwd:
                    em.dma(hb_s[n0:n0 + NB, :, tok0:tok0 + NT].rearrange("c p t -> p c t"), hh[:], reads=[b_hh], writes=[Buf()])
                    return
                hbl, b_hbl = HBL[k % 2]
                y, b_y = stc["y"], stc["b_y"]
                em.op("pool", lambda h, hh=hh, hbl=hbl: h.tensor_tensor(out=hh[:], in0=hh[:], in1=hbl[:], op=ALU.add), reads=[b_hh, b_hbl], writes=[b_hh])
                em.op("pool", lambda h, n0=n0, y=y, hh=hh, sg=sg: h.tensor_tensor(out=y[:, n0:n0 + NB, :], in0=hh[:], in1=sg[:], op=ALU.mult),
                      reads=[b_hh, b_sg], writes=[b_y])
                if g == NG - 1:
                    em.dma(in1_s[:, :, tok0:tok0 + NT].rearrange("c p t -> p c t"), y[:], reads=[b_y], writes=[Buf()])

            order = ((stage_Ae, 1), (stage_D, 3), (stage_A, 0), (stage_B, 1), (stage_C, 2))
            for step in range(len(items) + 3):
                for fn, lag in order:
                    idx = step - lag
                    if 0 <= idx < len(items):
                        fn(items[idx])
            em.barrier_all()

    em.barrier_all()
    plist = [lambda: pass_gla(1), lambda: pass_gla(0), pass_sconv, lambda: pass_out(0),
             lambda: pass_rglru(1), lambda: pass_rglru(0), lambda: pass_out(1)]
    for pi_, p_ in enumerate(plist[:npass]):
        em.mark("PASS%d" % pi_)
        p_()
    em.mark("END")
    em.finish()
    gs.close()
    return nc, em


_CACHE = {}


def make_in_maps(inp, n_cores=8):
    B = inp["x"].shape[0]
    maps = []
    shared = {
        "w_mod": np.ascontiguousarray(inp["w_mod"], np.float32),
        "e_w_in": np.ascontiguousarray(inp["e_w_in"][0], np.float32),
        "e_w_a2": np.ascontiguousarray(inp["e_w_a2"][0], np.float32),
        "e_w_out": np.ascontiguousarray(inp["e_w_out"][0], np.float32),
        "o_w_in": np.ascontiguousarray(inp["o_w_in"][0], np.float32),
        "o_w_a": np.ascontiguousarray(inp["o_w_a"][0], np.float32),
        "o_w_x": np.ascontiguousarray(inp["o_w_x"][0], np.float32),
        "o_w_out": np.ascontiguousarray(inp["o_w_out"][0], np.float32),
    }
    for core in range(n_cores):
        b = core % B
        m = dict(shared)
        m["xin"] = np.ascontiguousarray(np.concatenate([inp["ctx"][b], inp["x"][b]], axis=0), np.float32)
        m["vecs"] = pack_vecs(b, inp)
        maps.append(m)
    return maps


def kernel(**inputs):
    inp = {k: np.asarray(v) for k, v in inputs.items()}
    B, L, _ = inp["x"].shape
    CTX = inp["ctx"].shape[1]
    key = (L, CTX)
    if key not in _CACHE:
        _CACHE[key] = build(L, CTX)[0]
    nc = _CACHE[key]
    maps = make_in_maps(inp, 8)
    res = run_bass_kernel_spmd(nc, maps, core_ids=list(range(8)))
    out = np.stack([np.asarray(res.results[b]["out"], np.float32) for b in range(B)], axis=0)
    return out
```

```python
import numpy as np
from contextlib import ExitStack
import concourse.bass as bass
import concourse.mybir as mybir
from concourse.bass_utils import run_bass_kernel_spmd

F32 = mybir.dt.float32
BF16 = mybir.dt.bfloat16
AF = mybir.ActivationFunctionType
ALU = mybir.AluOpType

D = 1024
KC = 8
EPS = 1e-6
EVEN_IN = 7200
C_Q, C_K, C_V, C_GA, C_AF, C_AB, C_CB, C_CC, C_CX, C_GB = 0, 512, 1024, 2048, 3072, 3088, 3104, 4128, 5152, 6176


class Buf:
    __slots__ = ("name", "w", "r", "excl")

    def __init__(self, name="", excl=False):
        self.name = name
        self.w = None
        self.r = {}
        self.excl = excl


class Cut(Exception):
    pass


class Emitter:
    ENGS = ("pe", "act", "dve", "pool", "sp")
    max_inst = None
    marks = None

    def mark(self, name):
        if self.marks is not None:
            self.marks.append((name, self.n_inst, dict(self.raw)))

    def _chk(self):
        return self.max_inst is not None and self.n_inst >= self.max_inst

    def __init__(self, nc, n_chan=24):
        self.nc = nc
        self.prog = {e: [] for e in self.ENGS}
        self.sems, self.cnt, self.mult = {}, {}, {}
        self._ctx = []
        for e in self.ENGS:
            self._newunit(e, 1)
        self.chans = []
        for i in range(n_chan):
            u = "ch%d" % i
            self._newunit(u, 16)
            self.chans.append(u)
        self.seen = {e: {} for e in self.ENGS}
        self.chan_rr = 0
        self.n_inst = 0
        self.n_wait = 0
        self.raw = {e: 0 for e in self.ENGS}

    def _newunit(self, u, mult):
        cm = self.nc.semaphore("s_" + u)
        s = cm.__enter__()
        self._ctx.append(cm)
        self.sems[u] = s
        self.cnt[u] = 0
        self.mult[u] = mult

    def _deps(self, eng, reads, writes):
        deps = {}

        def add(uc, same_ok):
            if uc is None:
                return
            u, c = uc
            if u == eng and same_ok:
                return
            if deps.get(u, 0) < c:
                deps[u] = c

        pe = (eng == "pe")
        for b in reads:
            add(b.w, pe)
            if b.excl:
                for u, c in b.r.items():
                    if u != eng:
                        add((u, c), False)
        for b in writes:
            add(b.w, True)
            for u, c in b.r.items():
                add((u, c), pe)
        return deps

    def _emit_waits(self, eng, deps):
        seen = self.seen[eng]
        for u, c in deps.items():
            if seen.get(u, 0) >= c:
                continue
            seen[u] = c
            sem = self.sems[u]
            val = c * self.mult[u]
            self.prog[eng].append(lambda h, sem=sem, val=val: h.wait_ge(sem, val))
            self.n_wait += 1

    def _mark(self, unit, reads, writes):
        c = self.cnt[unit]
        for b in writes:
            b.w = (unit, c)
            b.r = {}
        for b in reads:
            if b.r.get(unit, 0) < c:
                b.r[unit] = c

    @staticmethod
    def _flat(bufs):
        out = []
        for b in bufs:
            if isinstance(b, (list, tuple)):
                out.extend(Emitter._flat(b))
            else:
                out.append(b)
        return out

    def op(self, eng, fn, reads=(), writes=(), signal=True):
        if self._chk():
            return
        reads, writes = self._flat(reads), self._flat(writes)
        deps = self._deps(eng, reads, writes)
        self._emit_waits(eng, deps)
        self.raw[eng] += 1
        self.cnt[eng] += 1
        if signal:
            sem = self.sems[eng]
            self.prog[eng].append(lambda h, fn=fn, sem=sem: fn(h).then_inc(sem, 1))
            self._mark(eng, reads, writes)
        else:
            self.prog[eng].append(lambda h, fn=fn: fn(h))
            self._mark(eng, reads, writes)
            self.cnt[eng] -= 1
        self.n_inst += 1

    def dma(self, out, in_, reads=(), writes=(), queue="sp"):
        if self._chk():
            return
        reads, writes = self._flat(reads), self._flat(writes)
        chan = self.chans[self.chan_rr % len(self.chans)]
        self.chan_rr += 1
        deps = self._deps(chan, reads, writes)
        if self.cnt[chan] > 0:
            deps[chan] = max(deps.get(chan, 0), self.cnt[chan])
        self._emit_waits(queue, deps)
        self.cnt[chan] += 1
        sem = self.sems[chan]
        self.prog[queue].append(lambda h, out=out, in_=in_, sem=sem: h.dma_start(out=out, in_=in_).then_inc(sem, 16))
        self._mark(chan, reads, writes)
        self.n_inst += 1

    def barrier_all(self):
        for e in self.ENGS:
            deps = {u: c for u, c in self.cnt.items() if c > 0 and u != e}
            if e != "pe" and self.cnt[e] > 0:
                deps[e] = self.cnt[e]
            self._emit_waits(e, deps)

    def finish(self):
        self.barrier_all()
        nc = self.nc
        prog = self.prog
        with nc.Block() as block:
            @block.tensor
            def _(h):
                for t in prog["pe"]:
                    t(h)

            @block.scalar
            def _(h):
                for t in prog["act"]:
                    t(h)

            @block.vector
            def _(h):
                for t in prog["dve"]:
                    t(h)

            @block.gpsimd
            def _(h):
                for t in prog["pool"]:
                    t(h)

            @block.sync
            def _(h):
                for t in prog["sp"]:
                    t(h)
        for cm in reversed(self._ctx):
            cm.__exit__(None, None, None)


VEC_SPEC = [("c", 8), ("cctx", 8), ("ng0", 8), ("ng1", 8), ("bm0", 24), ("bm1", 24),
            ("ba2_0", 4), ("ba2_1", 4), ("glag", 2), ("cw0", 8), ("cw1", 8), ("cw2", 8)]
for _d in range(2):
    for _j in range(4):
        VEC_SPEC.append(("ocw%d%d" % (_d, _j), 16))
    VEC_SPEC += [("ocb%d" % _d, 16), ("oba%d" % _d, 16), ("obx%d" % _d, 16), ("lam%d" % _d, 16)]
VEC_SPEC.append(("fg", 8))
VEC_OFF = {}
_o = 0
for _n, _w in VEC_SPEC:
    VEC_OFF[_n] = (_o, _w)
    _o += _w
NVEC = _o


def pack_vecs(b, inp):
    def fm(v):
        v = np.asarray(v, np.float32).reshape(-1)
        return v.reshape(v.size // 128, 128).T

    parts = {"c": inp["c"][b], "cctx": inp["c_ctx"], "ng0": inp["norm_g"][0], "ng1": inp["norm_g"][1],
             "bm0": inp["b_mod"][0], "bm1": inp["b_mod"][1], "ba2_0": inp["e_b_a2"][0, 0],
             "ba2_1": inp["e_b_a2"][0, 1], "glag": inp["e_gla_g"][0], "cw0": inp["e_conv_w"][0, 0],
             "cw1": inp["e_conv_w"][0, 1], "cw2": inp["e_conv_w"][0, 2], "fg": inp["final_g"]}
    for d in range(2):
        for j in range(4):
            parts["ocw%d%d" % (d, j)] = inp["o_conv_w"][0, d, j]
        parts["ocb%d" % d] = inp["o_conv_b"][0, d]
        parts["oba%d" % d] = inp["o_b_a"][0, d]
        parts["obx%d" % d] = inp["o_b_x"][0, d]
        parts["lam%d" % d] = inp["o_lam"][0, d]
    out = np.zeros((128, NVEC), np.float32)
    for n, w in VEC_SPEC:
        o, _ = VEC_OFF[n]
        out[:, o:o + w] = fm(parts[n])
    return out


def build(L, CTX, NT=256, dbg=False, npass=7, max_inst=None):
    assert CTX % NT == 0 and L % NT == 0 and NT % 128 == 0 and NT <= 512
    TOK = CTX + L
    NJ = NT // 128
    nc = bass.Bass("TRN2", target_bir_lowering=False)
    dt = nc.dram_tensor
    xin = dt("xin", [TOK, D], F32, kind="ExternalInput").ap()
    vecs_d = dt("vecs", [128, NVEC], F32, kind="ExternalInput").ap()
    w_mod = dt("w_mod", [2, D, 3 * D], F32, kind="ExternalInput").ap()
    e_w_in = dt("e_w_in", [D, EVEN_IN], F32, kind="ExternalInput").ap()
    e_w_a2 = dt("e_w_a2", [2, 16, 512], F32, kind="ExternalInput").ap()
    e_w_out = dt("e_w_out", [2 * D, D], F32, kind="ExternalInput").ap()
    o_w_in = dt("o_w_in", [D, 4 * D], F32, kind="ExternalInput").ap()
    o_w_a = dt("o_w_a", [2, 16, 128, 128], F32, kind="ExternalInput").ap()
    o_w_x = dt("o_w_x", [2, 16, 128, 128], F32, kind="ExternalInput").ap()
    o_w_out = dt("o_w_out", [2 * D, D], F32, kind="ExternalInput").ap()
    out_d = dt("out", [L, D], F32, kind="ExternalOutput").ap()
    sk = "ExternalOutput" if dbg else "Internal"
    ob_s = dt("ob_s", [8, 128, TOK], F32, kind=sk).ap()
    in0_s = dt("in0_s", [16, 128, TOK], BF16, kind=sk).ap()
    x1_s = dt("x1_s", [TOK, D], F32, kind=sk).ap()
    hb_s = dt("hb_s", [16, 128, TOK], F32, kind=sk).ap()
    in1_s = dt("in1_s", [16, 128, TOK], BF16, kind=sk).ap()

    em = Emitter(nc)
    em.max_inst = max_inst
    em.marks = []
    gs = ExitStack()

    uid = [0]

    def mk(stack, name, shape, dtype=F32, psum=False):
        uid[0] += 1
        name = "%s_%d" % (name, uid[0])
        t = stack.enter_context((nc.psum_tensor if psum else nc.sbuf_tensor)(name, shape, dtype))
        return t, Buf(name)

    banks = [mk(gs, "bank%d" % i, [128, 512], F32, psum=True) for i in range(8)]
    for _t, _b in banks:
        _b.excl = True

    ident, b_ident = mk(gs, "ident", [128, 128])
    maskF, b_maskF = mk(gs, "maskF", [128, 128])
    maskB, b_maskB = mk(gs, "maskB", [128, 128])
    ones_f, b_ones_f = mk(gs, "ones_f", [128, 128])
    ones_b, b_ones_b = mk(gs, "ones_b", [128, 128], BF16)
    ident_b, b_ident_b = mk(gs, "ident_b", [128, 128], BF16)
    rmask, b_rmask = mk(gs, "rmask", [128, 520])
    vecs, b_vecs = mk(gs, "vecs_sb", [128, NVEC])
    dv, b_dv = mk(gs, "dv", [128, 256])
    modT, b_modT = mk(gs, "modT", [128, 2, 48])
    stage = [mk(gs, "stage%d" % i, [128, 4096]) for i in range(2)]

    def V(name, lo=0, n=None):
        o, w = VEC_OFF[name]
        if n is None:
            n = w - lo
        return vecs[:, o + lo:o + lo + n]

    DV = {}
    _p = [0]

    def dvalloc(name, n):
        DV[name] = (_p[0], n)
        _p[0] += n

    for nm, n in [("sv", 16), ("A00", 8), ("A01", 8), ("A10", 8), ("A11", 8), ("nb0", 4), ("nb1", 4),
                  ("c1_0", 16), ("c1_1", 16), ("c2_0", 16), ("c2_1", 16), ("tmp", 16)]:
        dvalloc(nm, n)

    def DVv(name, lo=0, n=None):
        o, w = DV[name]
        if n is None:
            n = w - lo
        return dv[:, o + lo:o + lo + n]

    em.dma(vecs[:], vecs_d, writes=[b_vecs])
    em.op("pool", lambda h: h.memset(ident[:], 1.0), writes=[b_ident])
    em.op("pool", lambda h: h.affine_select(out=ident[:], in_=ident[:], pattern=[[-1, 128]], compare_op=ALU.is_equal,
                                            fill=0.0, base=0, channel_multiplier=1), reads=[b_ident], writes=[b_ident])
    em.op("pool", lambda h: h.memset(maskF[:], 1.0), writes=[b_maskF])
    em.op("pool", lambda h: h.affine_select(out=maskF[:], in_=maskF[:], pattern=[[1, 128]], compare_op=ALU.is_ge,
                                            fill=0.0, base=0, channel_multiplier=-1), reads=[b_maskF], writes=[b_maskF])
    em.op("pool", lambda h: h.memset(maskB[:], 1.0), writes=[b_maskB])
    em.op("pool", lambda h: h.affine_select(out=maskB[:], in_=maskB[:], pattern=[[-1, 128]], compare_op=ALU.is_ge,
                                            fill=0.0, base=0, channel_multiplier=1), reads=[b_maskB], writes=[b_maskB])
    em.op("pool", lambda h: h.memset(ones_f[:], 1.0), writes=[b_ones_f])
    em.op("pool", lambda h: h.memset(ones_b[:], 1.0), writes=[b_ones_b])
    em.op("dve", lambda h: h.tensor_copy(out=ident_b[:], in_=ident[:]), reads=[b_ident], writes=[b_ident_b])
    em.op("pool", lambda h: h.memset(rmask[:], 1.0), writes=[b_rmask])
    for q in range(5):
        em.op("pool", lambda h, q=q: h.memset(rmask[:, q * 128:q * 128 + 1], 0.0), reads=[b_rmask], writes=[b_rmask])

    sv = DVv("sv")
    sv3 = sv.rearrange("p (k w) -> p k w", w=2)
    em.op("act", lambda h: h.activation(out=sv3[:, :, 0], in_=V("c"), func=AF.Silu), reads=[b_vecs], writes=[b_dv])
    em.op("act", lambda h: h.activation(out=sv3[:, :, 1], in_=V("cctx"), func=AF.Silu), reads=[b_vecs, b_dv], writes=[b_dv])
    for d in range(2):
        em.op("dve", lambda h, d=d: h.tensor_scalar(out=DVv("nb%d" % d), in0=V("ba2_%d" % d), scalar1=-1.0, scalar2=None,
                                                    op0=ALU.mult), reads=[b_vecs, b_dv], writes=[b_dv])
    for d in range(2):
        em.op("act", lambda h, d=d: h.activation(out=DVv("tmp"), in_=V("lam%d" % d), func=AF.Exp, scale=-1.0),
              reads=[b_vecs, b_dv], writes=[b_dv])
        em.op("act", lambda h, d=d: h.activation(out=DVv("tmp"), in_=DVv("tmp"), func=AF.Ln, bias=1.0),
              reads=[b_dv], writes=[b_dv])
        em.op("dve", lambda h, d=d: h.tensor_scalar(out=DVv("c1_%d" % d), in0=DVv("tmp"), scalar1=-8.0, scalar2=None,
                                                    op0=ALU.mult), reads=[b_dv], writes=[b_dv])
        em.op("dve", lambda h, d=d: h.tensor_scalar(out=DVv("c2_%d" % d), in0=DVv("tmp"), scalar1=-16.0, scalar2=None,
                                                    op0=ALU.mult), reads=[b_dv], writes=[b_dv])

    pm, b_pm = banks[7]
    for li in range(2):
        for cb in range(6):
            st, b_st = stage[cb % 2]
            st3 = st.rearrange("p (k n) -> p k n", n=512)
            for kc in range(KC):
                em.dma(st3[:, kc, :], w_mod[li, kc * 128:(kc + 1) * 128, cb * 512:(cb + 1) * 512], writes=[b_st])
            for mm in range(4):
                m = cb * 4 + mm
                for kc in range(KC):
                    em.op("pe", lambda h, m=m, mm=mm, kc=kc, st3=st3: h.matmul(
                        pm[:, m * 2:m * 2 + 2], lhsT=st3[:, kc, mm * 128:(mm + 1) * 128], rhs=sv3[:, kc, :],
                        start=(kc == 0), stop=(kc == KC - 1)), reads=[b_st, b_dv], writes=[b_pm])
        pm3 = pm[:, 0:48].rearrange("p (m w) -> p m w", w=2)
        mo3 = modT[:, li, :].rearrange("p (m w) -> p m w", w=2)
        for w_ in range(2):
            em.op("dve", lambda h, w_=w_, li=li, pm3=pm3, mo3=mo3: h.tensor_tensor(
                out=mo3[:, :, w_], in0=pm3[:, :, w_], in1=V("bm%d" % li), op=ALU.add),
                reads=[b_pm, b_vecs], writes=[b_modT])
        for w_ in range(2):
            em.op("dve", lambda h, w_=w_, li=li, mo3=mo3: h.scalar_tensor_tensor(
                out=DVv("A%d%d" % (li, w_)), in0=mo3[:, 8:16, w_], scalar=1.0, in1=V("ng%d" % li),
                op0=ALU.add, op1=ALU.mult), reads=[b_modT, b_vecs, b_dv], writes=[b_dv])

    def mod_vec(li, which, part):
        mo3 = modT[:, li, :].rearrange("p (m w) -> p m w", w=2)
        return mo3[:, part * 8:(part + 1) * 8, which]

    eng_rr = [0]

    def cast_eng():
        e = ("dve", "act", "pool")[eng_rr[0] % 3]
        eng_rr[0] += 1
        return e

    def copy_op(eng, out, in_, reads, writes):
        if eng == "act":
            em.op("act", lambda h: h.activation(out=out, in_=in_, func=AF.Copy), reads=reads, writes=writes)
        else:
            em.op(eng, lambda h: h.tensor_copy(out=out, in_=in_), reads=reads, writes=writes)

    ld_rr = [0]

    def load_cast(dst, b_dst, src, ncols):
        c0 = 0
        while c0 < ncols:
            n = min(4096, ncols - c0)
            st, b_st = stage[ld_rr[0] % 2]
            ld_rr[0] += 1
            em.dma(st[:, 0:n], src[:, c0:c0 + n], writes=[b_st])
            if isinstance(b_dst, list):
                nb_ = Buf("wpart")
                b_dst.append(nb_)
                copy_op(cast_eng(), dst[:, c0:c0 + n], st[:, 0:n], [b_st], [nb_])
            else:
                copy_op(cast_eng(), dst[:, c0:c0 + n], st[:, 0:n], [b_st], [b_dst])
            c0 += n

    def bcast_row(stack, name, vec_ap, b_src):
        t, b_t = mk(stack, name, [128, 1024])
        dg, b_dg = mk(stack, name + "_dg", [128, 128])
        for kc in range(KC):
            bk, b_bk = banks[6 + (kc // 4)]
            em.op("dve", lambda h, kc=kc: h.tensor_scalar(out=dg[:], in0=ident[:], scalar1=vec_ap[:, kc:kc + 1], scalar2=None,
                                                          op0=ALU.mult), reads=[b_ident, b_src], writes=[b_dg])
            em.op("pe", lambda h, kc=kc, bk=bk: h.matmul(bk[:, (kc % 4) * 128:(kc % 4 + 1) * 128], lhsT=ones_f[:], rhs=dg[:],
                                                         start=True, stop=True), reads=[b_ones_f, b_dg], writes=[b_bk])
            em.op("act", lambda h, kc=kc, bk=bk: h.activation(out=t[:, kc * 128:(kc + 1) * 128],
                                                              in_=bk[:, (kc % 4) * 128:(kc % 4 + 1) * 128], func=AF.Copy),
                  reads=[b_bk], writes=[b_t])
        return t, b_t

    def supertiles(reverse):
        ctx_t = [(t0, True) for t0 in range(0, CTX, NT)]
        lat_t = [(t0, False) for t0 in range(CTX, TOK, NT)]
        if reverse:
            ctx_t, lat_t = ctx_t[::-1], lat_t[::-1]
        res = []
        for seq in (ctx_t, lat_t):
            for i, (t0, c) in enumerate(seq):
                res.append((t0, c, i == 0))
        return res

    class Front:
        def __init__(self, stack, xsrc, li):
            self.xsrc, self.li = xsrc, li
            self.xt = [mk(stack, "f_xt%d" % i, [128, D]) for i in range(2)]
            self.xn = [mk(stack, "f_xn%d" % i, [128, D], BF16) for i in range(2)]
            self.junk = mk(stack, "f_junk", [128, D], BF16)
            self.ss = [mk(stack, "f_ss%d" % i, [128, 1]) for i in range(2)]
            self.hT = [mk(stack, "f_hT%d" % i, [128, KC, NT], BF16) for i in range(2)]
            self.hT = [(t, (Buf("hTa"), Buf("hTd"))) for (t, _) in self.hT]
            self.n = 0
            self.k = 0

        def run(self, tok0, is_ctx):
            self.load(tok0)
            return self.finish(tok0, is_ctx)

        def load(self, tok0):
            assert NJ == 2
            self.pending = []
            for j in range(NJ):
                xt, b_xt = self.xt[j]
                xn, b_xn = self.xn[j]
                ss, b_ss = self.ss[j]
                junk, b_junk = self.junk
                self.pending.append((xn, b_xn))
                r0 = tok0 + j * 128
                em.dma(xt[:], self.xsrc[r0:r0 + 128, :], writes=[b_xt])
                em.op("act", lambda h, xt=xt, ss=ss: h.activation(out=junk[:], in_=xt[:], func=AF.Square, accum_out=ss[:]),
                      reads=[b_xt], writes=[b_junk, b_ss])
                em.op("dve", lambda h, ss=ss: h.tensor_scalar(out=ss[:], in0=ss[:], scalar1=1.0 / D, scalar2=EPS,
                                                              op0=ALU.mult, op1=ALU.add), reads=[b_ss], writes=[b_ss])
                em.op("act", lambda h, ss=ss: h.activation(out=ss[:], in_=ss[:], func=AF.Ln), reads=[b_ss], writes=[b_ss])
                em.op("act", lambda h, ss=ss: h.activation(out=ss[:], in_=ss[:], func=AF.Exp, scale=-0.5), reads=[b_ss], writes=[b_ss])
                em.op("pool", lambda h, xt=xt, xn=xn, ss=ss: h.tensor_scalar(out=xn[:], in0=xt[:], scalar1=ss[:, 0:1], scalar2=0.0,
                                                                             op0=ALU.mult, op1=ALU.add),
                      reads=[b_xt, b_ss], writes=[b_xn])

        def finish(self, tok0, is_ctx):
            hT, b_hT = self.hT[self.n % 2]
            self.n += 1
            w_ = 1 if is_ctx else 0
            A = DVv("A%d%d" % (self.li, w_))
            sh = mod_vec(self.li, w_, 0)
            for j in range(NJ):
                xn, b_xn = self.pending[j]
                bk, b_bk = banks[j % 2]
                bkb = bk[:, :].bitcast(BF16)
                for kc in range(KC):
                    em.op("pe", lambda h, xn=xn, kc=kc, bkb=bkb: h.transpose(
                        out=bkb[:, kc * 128:(kc + 1) * 128], in_=xn[:, kc * 128:(kc + 1) * 128], identity=ident_b[:]),
                        reads=[b_xn, b_ident_b], writes=[b_bk], signal=(kc == KC - 1))
                for kc in range(KC):
                    eng = "act" if j % 2 == 0 else "dve"
                    if eng == "act":
                        em.op("act", lambda h, kc=kc, bkb=bkb, j=j: h.activation(
                            out=hT[:, kc, j * 128:(j + 1) * 128], in_=bkb[:, kc * 128:(kc + 1) * 128], func=AF.Identity,
                            scale=A[:, kc:kc + 1], bias=sh[:, kc:kc + 1]), reads=[b_bk, b_dv, b_modT], writes=[b_hT[0]])
                    else:
                        em.op("dve", lambda h, kc=kc, bkb=bkb, j=j: h.tensor_scalar(
                            out=hT[:, kc, j * 128:(j + 1) * 128], in0=bkb[:, kc * 128:(kc + 1) * 128],
                            scalar1=A[:, kc:kc + 1], scalar2=sh[:, kc:kc + 1], op0=ALU.mult, op1=ALU.add),
                            reads=[b_bk, b_dv, b_modT], writes=[b_hT[1]])
            return hT, b_hT

    def proj_fm(bank_ap, b_bank, W, col0, hT, b_hT, b_W, M=128):
        for kc in range(KC):
            em.op("pe", lambda h, kc=kc: h.matmul(bank_ap, lhsT=W[:, kc, col0:col0 + M], rhs=hT[:, kc, :],
                                                  start=(kc == 0), stop=(kc == KC - 1)),
                  reads=[b_W, b_hT], writes=[b_bank], signal=(kc == KC - 1))

    def pass_gla(d):
        with ExitStack() as ps:
            fwd = (d == 0)
            ncol = 3072 if fwd else 2048
            W, b_W = mk(ps, "g_W", [128, KC, ncol], BF16)
            b_W = []
            Wa, b_Wa = mk(ps, "g_Wa", [128, KC, 16], BF16)
            b_Wa = []
            wa2, b_wa2 = mk(ps, "g_wa2", [16, 512])
            for kc in range(KC):
                load_cast(W[:, kc, :], b_W, e_w_in[kc * 128:(kc + 1) * 128, 0:ncol], ncol)
                ca = C_AF if fwd else C_AB
                load_cast(Wa[:, kc, :], b_Wa, e_w_in[kc * 128:(kc + 1) * 128, ca:ca + 16], 16)
            em.dma(wa2[:], e_w_a2[d], writes=[b_wa2])
            fr = Front(ps, xin, 0)
            v_sb, _ = mk(ps, "g_v", [128, NJ, 1024], BF16)
            b_v = (Buf("v_h0"), Buf("v_h1"))
            a_sb, b_a = mk(ps, "g_a", [16, NT])
            hb = []
            for i in range(4):
                hb.append(dict(sp=mk(ps, "g_sp%d" % i, [128, NT]), c=mk(ps, "g_c%d" % i, [128, NT]),
                               E1=mk(ps, "g_E1%d" % i, [128, NT]), E2=mk(ps, "g_E2%d" % i, [128, NT]),
                               qi=mk(ps, "g_qi%d" % i, [128, NT], BF16), ki=mk(ps, "g_ki%d" % i, [128, NT], BF16)))
            ktok = [mk(ps, "g_kt%d" % i, [128, 128], BF16) for i in range(4)]
            scT = [mk(ps, "g_sc%d" % i, [128, 128], BF16) for i in range(4)]
            S = [mk(ps, "g_S%d" % h_, [128, 256]) for h_ in range(4)]
            Sb = [mk(ps, "g_Sb%d" % h_, [128, 256], BF16) for h_ in range(4)]
            tmpS = [mk(ps, "g_tS%d" % i, [128, 256]) for i in range(4)]
            o_sb, b_o = mk(ps, "g_o", [128, 8, NT])
            if fwd:
                obl, b_obl = mk(ps, "g_obl", [128, 8, NT])
                sga, b_sga = mk(ps, "g_sga", [128, 8, NT])
                sq, b_sq = mk(ps, "g_sq", [128, 8, NT], BF16)
                rstd, b_rstd = mk(ps, "g_rstd", [128, 4, NT])
                og, b_og = mk(ps, "g_og", [128, 8, NT], BF16)
            for h_ in range(4):
                em.op("pool", lambda h, h_=h_: h.memset(S[h_][0][:], 0.0), writes=[S[h_][1]])
                em.op("pool", lambda h, h_=h_: h.memset(Sb[h_][0][:], 0.0), writes=[Sb[h_][1]])
            assert NT == 256
            PQK, PZ, PKT, PSI = banks[2], banks[3], banks[6], banks[7]
            POs = (banks[4], banks[5])
            b_sc, b_kt, b_inc = PSI[1], PKT[1], PSI[1]
            ps_kt = PKT[0][:, 0:64].bitcast(BF16)
            ps_sc = PSI[0][:, 0:128]
            ps_inc = PSI[0][:, 256:512]
            step_n = 0
            mask, b_mask = (maskF, b_maskF) if fwd else (maskB, b_maskB)
            cnt = 0
            sts = supertiles(reverse=not fwd)
            nxt = fr.run(sts[0][0], sts[0][1])
            for sidx, (tok0, is_ctx, first) in enumerate(sts):
                hT, b_hT = nxt
                em.mark("gla%d st%d v" % (d, tok0))
                v_banks = (PZ, PKT, POs[0], POs[1])
                for j in range(NJ):
                    for half in range(2):
                        pv, b_pv = v_banks[(j * 2 + half) % 4]
                        for kc in range(KC):
                            em.op("pe", lambda h, kc=kc, j=j, half=half, pv=pv, hT=hT: h.matmul(
                                pv[:, :], lhsT=hT[:, kc, j * 128:(j + 1) * 128],
                                rhs=W[:, kc, C_V + half * 512:C_V + (half + 1) * 512], start=(kc == 0), stop=(kc == KC - 1)),
                                reads=[b_W, b_hT], writes=[b_pv], signal=(kc == KC - 1))
                        copy_op("act" if half == 0 else "dve", v_sb[:, j, half * 512:(half + 1) * 512], pv[:, :], [b_pv], [b_v[half]])
                em.mark("gla%d st%d a" % (d, tok0))
                pa, b_pa = PZ
                proj_fm(pa[0:16, 0:NT], b_pa, Wa, 0, hT, b_hT, b_Wa, M=16)
                copy_op("act", a_sb[:], pa[0:16, 0:NT], [b_pa], [b_a])
                if fwd:
                    em.dma(obl[:], ob_s[:, :, tok0:tok0 + NT].rearrange("c p t -> p c t"), writes=[b_obl])
                nb = DVv("nb%d" % d)
                z_banks = (PZ, PKT)
                qk_banks = (PQK, POs[0], POs[1], PSI)
                zregs = []
                for h_ in range(4):
                    zb = z_banks[h_ // 2]
                    zreg = zb[0][:, (h_ % 2) * 256:(h_ % 2) * 256 + NT]
                    zregs.append((zreg, zb[1]))
                    em.op("pe", lambda h, h_=h_, zreg=zreg: h.matmul(zreg, lhsT=wa2[:, h_ * 128:(h_ + 1) * 128], rhs=a_sb[:],
                                                                    start=True, stop=True), reads=[b_wa2, b_a], writes=[zb[1]])
                for h_ in range(4):
                    B = hb[h_]
                    (sp, b_sp), (c_, b_c), (E1, b_E1), (E2, b_E2) = B["sp"], B["c"], B["E1"], B["E2"]
                    zreg, b_pz = zregs[h_]
                    em.op("act", lambda h, h_=h_, sp=sp, zreg=zreg: h.activation(out=sp[:], in_=zreg, func=AF.Exp, scale=-1.0,
                                                                                bias=nb[:, h_:h_ + 1]), reads=[b_pz, b_dv], writes=[b_sp])
                    em.op("act", lambda h, sp=sp: h.activation(out=sp[:], in_=sp[:], func=AF.Ln, bias=1.0), reads=[b_sp], writes=[b_sp])
                    if fwd:
                        em.op("dve", lambda h, sp=sp, c_=c_: h.tensor_tensor_scan(
                            out=c_[:], data0=rmask[:, 0:NT], data1=sp[:], initial=0.0, op0=ALU.mult, op1=ALU.add),
                            reads=[b_sp, b_rmask], writes=[b_c])
                    else:
                        em.op("dve", lambda h, sp=sp, c_=c_: h.tensor_tensor_scan(
                            out=c_[:, ::-1], data0=rmask[:, 1:NT + 1][:, ::-1], data1=sp[:, ::-1], initial=0.0,
                            op0=ALU.mult, op1=ALU.add), reads=[b_sp, b_rmask], writes=[b_c])
                    em.op("act", lambda h, c_=c_, E1=E1: h.activation(out=E1[:], in_=c_[:], func=AF.Exp, scale=-1.0 / 16), reads=[b_c], writes=[b_E1])
                    em.op("act", lambda h, c_=c_, E2=E2: h.activation(out=E2[:], in_=c_[:], func=AF.Exp, scale=1.0 / 16), reads=[b_c], writes=[b_E2])
                for h_ in range(4):
                    qb = qk_banks[h_]
                    proj_fm(qb[0][:, 0:NT], qb[1], W, C_Q + h_ * 128, hT, b_hT, b_W)
                    proj_fm(qb[0][:, 256:256 + NT], qb[1], W, C_K + h_ * 128, hT, b_hT, b_W)
                for h_ in range(4):
                    B = hb[h_]
                    (E1, b_E1), (E2, b_E2), (qi, b_qi), (ki, b_ki) = B["E1"], B["E2"], B["qi"], B["ki"]
                    qb = qk_banks[h_]
                    pq, pk, b_pq = qb[0][:, 0:NT], qb[0][:, 256:256 + NT], qb[1]
                    em.op("dve", lambda h, qi=qi, E1=E1, pq=pq: h.scalar_tensor_tensor(
                        out=qi[:], in0=pq, scalar=128.0 ** -0.5, in1=E1[:], op0=ALU.mult, op1=ALU.mult),
                        reads=[b_pq, b_E1], writes=[b_qi])
                    em.op("dve", lambda h, ki=ki, E2=E2, pk=pk: h.tensor_tensor(out=ki[:], in0=pk, in1=E2[:], op=ALU.mult),
                          reads=[b_pq, b_E2], writes=[b_ki])
                if sidx + 1 < len(sts):
                    fr.load(sts[sidx + 1][0])
                js = range(NJ) if fwd else range(NJ - 1, -1, -1)
                for j in js:
                    for h_ in range(4):
                        B = hb[h_]
                        (E1, b_E1), (qi, b_qi), (ki, b_ki) = B["E1"], B["qi"], B["ki"]
                        tsl = slice(j * 128, (j + 1) * 128)
                        kt, b_ktok = ktok[h_]
                        sc, b_scT = scT[h_]
                        tS, b_tS = tmpS[h_]
                        PO = POs[step_n % 2]
                        step_n += 1
                        St, b_S = S[h_]
                        Sbt, b_Sb = Sb[h_]
                        em.op("pe", lambda h, ki=ki, tsl=tsl: h.transpose(out=ps_kt, in_=ki[:, tsl], identity=ident_b[:]),
                              reads=[b_ki, b_ident_b], writes=[b_kt])
                        copy_op("act", kt[:], ps_kt, [b_kt], [b_ktok])
                        em.op("pe", lambda h, ki=ki, qi=qi, tsl=tsl: h.matmul(ps_sc, lhsT=ki[:, tsl], rhs=qi[:, tsl],
                                                                             start=True, stop=True),
                              reads=[b_ki, b_qi], writes=[b_sc])
                        em.op("dve", lambda h, sc=sc: h.tensor_tensor(out=sc[:], in0=ps_sc, in1=mask[:], op=ALU.mult),
                              reads=[b_sc, b_mask], writes=[b_scT])
                        for c in range(2):
                            po, b_po = PO[0][:, c * 128:(c + 1) * 128], PO[1]
                            vs = slice(h_ * 256 + c * 128, h_ * 256 + (c + 1) * 128)
                            em.op("pe", lambda h, po=po, vs=vs, sc=sc, j=j: h.matmul(
                                po, lhsT=v_sb[:, j, vs], rhs=sc[:], start=True, stop=False),
                                reads=[b_v, b_scT], writes=[b_po], signal=False)
                            em.op("pe", lambda h, po=po, c=c, Sbt=Sbt, qi=qi, tsl=tsl: h.matmul(
                                po, lhsT=Sbt[:, c * 128:(c + 1) * 128], rhs=qi[:, tsl], start=False, stop=True),
                                reads=[b_Sb, b_qi], writes=[b_po])
                        em.op("pe", lambda h, kt=kt, j=j, h_=h_: h.matmul(ps_inc, lhsT=kt[:], rhs=v_sb[:, j, h_ * 256:(h_ + 1) * 256],
                                                                          start=True, stop=True), reads=[b_ktok, b_v], writes=[b_inc])
                        dcol = j * 128 + (127 if fwd else 0)
                        em.op("dve", lambda h, tS=tS, St=St: h.tensor_tensor(out=tS[:], in0=ps_inc, in1=St[:], op=ALU.add),
                              reads=[b_inc, b_S], writes=[b_tS])
                        em.op("act", lambda h, tS=tS, St=St, E1=E1, dcol=dcol: h.activation(out=St[:], in_=tS[:], func=AF.Copy,
                                                                                          scale=E1[:, dcol:dcol + 1]),
                              reads=[b_tS, b_E1], writes=[b_S])
                        em.op("pool", lambda h, tS=tS, Sbt=Sbt, E1=E1, dcol=dcol: h.tensor_scalar(
                            out=Sbt[:], in0=tS[:], scalar1=E1[:, dcol:dcol + 1], scalar2=0.0, op0=ALU.mult, op1=ALU.add),
                            reads=[b_tS, b_E1], writes=[b_Sb])
                        po3 = PO[0][:, 0:256].rearrange("p (c t) -> p c t", c=2)
                        b_po = PO[1]
                        if fwd:
                            em.op("dve", lambda h, po3=po3, h_=h_, tsl=tsl: h.tensor_tensor(
                                out=o_sb[:, 2 * h_:2 * h_ + 2, tsl], in0=po3, in1=obl[:, 2 * h_:2 * h_ + 2, tsl], op=ALU.add),
                                reads=[b_po, b_obl], writes=[b_o])
                        else:
                            copy_op("act", o_sb[:, 2 * h_:2 * h_ + 2, tsl], po3, [b_po], [b_o])
                if sidx + 1 < len(sts):
                    nxt = fr.finish(sts[sidx + 1][0], sts[sidx + 1][1])
                if not fwd:
                    em.dma(ob_s[:, :, tok0:tok0 + NT].rearrange("c p t -> p c t"), o_sb[:], reads=[b_o], writes=[Buf()])
                    continue
                for cc in range(8):
                    pg, b_pg = (PQK, PSI)[cc % 2]
                    proj_fm(pg[:, 0:NT], b_pg, W, C_GA + cc * 128, hT, b_hT, b_W)
                    em.op("act", lambda h, pg=pg, cc=cc: h.activation(out=sga[:, cc, :], in_=pg[:, 0:NT], func=AF.Silu),
                          reads=[b_pg], writes=[b_sga])
                em.op("act", lambda h: h.activation(out=sq[:], in_=o_sb[:], func=AF.Square), reads=[b_o], writes=[b_sq])
                n_banks = (PZ, PKT)
                for h_ in range(4):
                    nbk = n_banks[h_ // 2]
                    reg = nbk[0][:, (h_ % 2) * 256:(h_ % 2) * 256 + NT]
                    for c in range(2):
                        em.op("pe", lambda h, c=c, h_=h_, reg=reg: h.matmul(reg, lhsT=ones_b[:], rhs=sq[:, 2 * h_ + c, :], start=(c == 0), stop=(c == 1)),
                              reads=[b_ones_b, b_sq], writes=[nbk[1]], signal=(c == 1))
                for q_ in range(2):
                    nbk = n_banks[q_]
                    src = nbk[0][:, 0:512].rearrange("p (n t) -> p n t", n=2)[:, :, 0:NT]
                    em.op("dve", lambda h, q_=q_, src=src: h.tensor_scalar(out=rstd[:, 2 * q_:2 * q_ + 2, :], in0=src, scalar1=1.0 / 256, scalar2=EPS,
                                                                          op0=ALU.mult, op1=ALU.add), reads=[nbk[1]], writes=[b_rstd])
                em.op("act", lambda h: h.activation(out=rstd[:], in_=rstd[:], func=AF.Ln), reads=[b_rstd], writes=[b_rstd])
                em.op("act", lambda h: h.activation(out=rstd[:], in_=rstd[:], func=AF.Exp, scale=-0.5), reads=[b_rstd], writes=[b_rstd])
                for ch in range(8):
                    c = ch % 2
                    em.op("dve", lambda h, ch=ch, c=c: h.scalar_tensor_tensor(
                        out=o_sb[:, ch, :], in0=o_sb[:, ch, :], scalar=V("glag", c, 1), in1=sga[:, ch, :], op0=ALU.mult, op1=ALU.mult),
                        reads=[b_o, b_sga, b_vecs, b_sq], writes=[b_o])
                for ch in range(8):
                    eng = "dve" if ch % 2 == 0 else "pool"
                    em.op(eng, lambda h, ch=ch: h.tensor_tensor(out=og[:, ch, :], in0=o_sb[:, ch, :], in1=rstd[:, ch // 2, :], op=ALU.mult),
                          reads=[b_o, b_rstd], writes=[b_og])
                em.dma(in0_s[0:8, :, tok0:tok0 + NT].rearrange("c p t -> p c t"), og[:], reads=[b_og], writes=[Buf()])
            em.barrier_all()

    def pass_sconv():
        with ExitStack() as ps:
            W, b_W = mk(ps, "c_W", [128, KC, 4096], BF16)
            b_W = []
            for kc in range(KC):
                load_cast(W[:, kc, :], b_W, e_w_in[kc * 128:(kc + 1) * 128, C_CB:C_CB + 4096], 4096)
            em.mark("sconv weights done")
            fr = Front(ps, xin, 0)
            tcc = [mk(ps, "c_tcc%d" % i, [128, NT]) for i in range(2)]
            sgb = [mk(ps, "c_sgb%d" % i, [128, NT]) for i in range(2)]
            z = [mk(ps, "c_z%d" % i, [128, NT]) for i in range(2)]
            zc = [mk(ps, "c_zc%d" % i, [128, NT]) for i in range(2)]
            y_sb = [mk(ps, "c_y%d" % i, [128, 8, NT], BF16) for i in range(2)]
            n = 0
            sts = supertiles(False)
            nxt = fr.run(sts[0][0], sts[0][1])
            for si, (tok0, is_ctx, first) in enumerate(sts):
                hT, b_hT = nxt
                RW = NT if is_ctx else 64
                assert (not is_ctx) or CTX == NT
                y, b_y = y_sb[si % 2]
                for cc in range(8):
                    em.mark("sconv st%d cc%d" % (tok0, cc))
                    if cc == 1 and si + 1 < len(sts):
                        fr.load(sts[si + 1][0])
                    i2 = n % 2
                    n += 1
                    bA, b_bA = banks[2 + 4 * i2]
                    bB, b_bB = banks[3 + 4 * i2]
                    regs = [(bB[:, 256:256 + NT], b_bB), (bA[:, 0:NT], b_bA), (bB[:, 0:NT], b_bB), (bA[:, 256:256 + NT], b_bA)]
                    for qi_, col in ((1, 1024), (3, 3072), (2, 2048), (0, 0)):
                        proj_fm(regs[qi_][0], regs[qi_][1], W, col + cc * 128, hT, b_hT, b_W)
                    (t_, b_t), (sg, b_sg), (z_, b_z), (zc_, b_zc) = tcc[i2], sgb[i2], z[i2], zc[i2]
                    em.op("act", lambda h, t_=t_, r=regs[1][0]: h.activation(out=t_[:], in_=r, func=AF.Copy), reads=[b_bA], writes=[b_t])
                    em.op("act", lambda h, sg=sg, r=regs[3][0]: h.activation(out=sg[:], in_=r, func=AF.Silu), reads=[b_bA], writes=[b_sg])
                    em.op("dve", lambda h, z_=z_, t_=t_, r=regs[2][0]: h.tensor_tensor(out=z_[:], in0=r, in1=t_[:], op=ALU.mult),
                          reads=[b_bB, b_t], writes=[b_z])
                    z3 = z_.rearrange("p (r w) -> p r w", w=RW)
                    zc3 = zc_.rearrange("p (r w) -> p r w", w=RW)
                    em.op("dve", lambda h, z_=z_, zc_=zc_, cc=cc: h.tensor_scalar(out=zc_[:], in0=z_[:], scalar1=V("cw1", cc, 1), scalar2=0.0,
                                                                                  op0=ALU.mult, op1=ALU.add), reads=[b_z, b_vecs], writes=[b_zc])
                    em.op("dve", lambda h, z3=z3, zc3=zc3, cc=cc, RW=RW: h.scalar_tensor_tensor(
                        out=zc3[:, :, 1:RW], in0=z3[:, :, 0:RW - 1], scalar=V("cw0", cc, 1), in1=zc3[:, :, 1:RW], op0=ALU.mult, op1=ALU.add),
                        reads=[b_z, b_zc, b_vecs], writes=[b_zc])
                    em.op("dve", lambda h, z3=z3, zc3=zc3, cc=cc, RW=RW: h.scalar_tensor_tensor(
                        out=zc3[:, :, 0:RW - 1], in0=z3[:, :, 1:RW], scalar=V("cw2", cc, 1), in1=zc3[:, :, 0:RW - 1], op0=ALU.mult, op1=ALU.add),
                        reads=[b_z, b_zc, b_vecs], writes=[b_zc])
                    em.op("dve", lambda h, zc_=zc_, r=regs[0][0]: h.tensor_tensor(out=zc_[:], in0=r, in1=zc_[:], op=ALU.mult),
                          reads=[b_bB, b_zc], writes=[b_zc])
                    em.op("dve", lambda h, zc_=zc_, sg=sg, y=y, cc=cc: h.tensor_tensor(out=y[:, cc, :], in0=zc_[:], in1=sg[:], op=ALU.mult),
                          reads=[b_zc, b_sg], writes=[b_y])
                if si + 1 < len(sts):
                    nxt = fr.finish(sts[si + 1][0], sts[si + 1][1])
                em.mark("sconv st%d store" % tok0)
                em.dma(in0_s[8:16, :, tok0:tok0 + NT].rearrange("c p t -> p c t"), y[:], reads=[b_y], writes=[Buf()])
            em.barrier_all()

    def pass_out(li):
        with ExitStack() as ps:
            last = (li == 1)
            W, b_W = mk(ps, "o_W", [128, 16, 1024], BF16)
            b_W = []
            wsrc = o_w_out if last else e_w_out
            for kc in range(16):
                load_cast(W[:, kc, :], b_W, wsrc[kc * 128:(kc + 1) * 128, :], 1024)
            g_bc, b_gbc = bcast_row(ps, "o_gbc", mod_vec(li, 0, 2), b_modT)
            if last:
                f_bc, b_fbc = bcast_row(ps, "o_fbc", V("fg"), b_vecs)
            else:
                gc_bc, b_gcbc = bcast_row(ps, "o_gcbc", mod_vec(li, 1, 2), b_modT)
            inn = [mk(ps, "o_in%d" % i, [128, 16, NT], BF16) for i in range(2)]
            xt = [mk(ps, "o_xt%d" % i, [128, D]) for i in range(2 * NJ)]
            xo = [mk(ps, "o_xo%d" % i, [128, D]) for i in range(2)]
            tmp = [mk(ps, "o_tmp%d" % i, [128, 512]) for i in range(2)]
            ss = [mk(ps, "o_ss%d" % i, [128, 1]) for i in range(2)]
            junk, b_junk = mk(ps, "o_junk", [128, D], BF16)
            src_in = in1_s if last else in0_s
            src_x = x1_s if last else xin
            k = 0
            n = 0
            sts = [st for st in supertiles(False) if not (last and st[1])]

            def issue_loads(si):
                tok0 = sts[si][0]
                it, b_it = inn[si % 2]
                em.dma(it[:], src_in[:, :, tok0:tok0 + NT].rearrange("c p t -> p c t"), writes=[b_it])
                for j in range(NJ):
                    x_, b_x = xt[(si % 2) * NJ + j]
                    r0 = tok0 + j * 128
                    em.dma(x_[:], src_x[r0:r0 + 128, :], writes=[b_x])

            issue_loads(0)
            for si, (tok0, is_ctx, first) in enumerate(sts):
                if si + 1 < len(sts):
                    issue_loads(si + 1)
                it, b_it = inn[si % 2]
                gb_, b_gb_ = (gc_bc, b_gcbc) if (is_ctx and not last) else (g_bc, b_gbc)
                for j in range(NJ):
                    x_, b_x = xt[(si % 2) * NJ + j]
                    xo_, b_xo = xo[k % 2]
                    ss_, b_ss = ss[k % 2]
                    k += 1
                    r0 = tok0 + j * 128
                    for half in range(2):
                        bk, b_bk = banks[2 + n % 4]
                        t_, b_t = tmp[n % 2]
                        n += 1
                        hs = slice(half * 512, (half + 1) * 512)
                        for kc in range(16):
                            em.op("pe", lambda h, kc=kc, j=j, hs=hs, bk=bk, it=it: h.matmul(
                                bk[:, :], lhsT=it[:, kc, j * 128:(j + 1) * 128], rhs=W[:, kc, hs], start=(kc == 0), stop=(kc == 15)),
                                reads=[b_it, b_W], writes=[b_bk], signal=(kc == 15))
                        em.op("dve", lambda h, bk=bk, t_=t_, hs=hs, gb_=gb_: h.tensor_tensor(out=t_[:], in0=bk[:, :], in1=gb_[:, hs], op=ALU.mult),
                              reads=[b_bk, b_gb_], writes=[b_t])
                        em.op("pool", lambda h, t_=t_, x_=x_, xo_=xo_, hs=hs: h.tensor_tensor(out=xo_[:, hs], in0=t_[:], in1=x_[:, hs], op=ALU.add),
                              reads=[b_t, b_x], writes=[b_xo])
                    if not last:
                        em.dma(x1_s[r0:r0 + 128, :], xo_[:], reads=[b_xo], writes=[Buf()])
                    else:
                        em.op("act", lambda h, xo_=xo_, ss_=ss_: h.activation(out=junk[:], in_=xo_[:], func=AF.Square, accum_out=ss_[:]),
                              reads=[b_xo], writes=[b_junk, b_ss])
                        em.op("dve", lambda h, ss_=ss_: h.tensor_scalar(out=ss_[:], in0=ss_[:], scalar1=1.0 / D, scalar2=EPS,
                                                                        op0=ALU.mult, op1=ALU.add), reads=[b_ss], writes=[b_ss])
                        em.op("act", lambda h, ss_=ss_: h.activation(out=ss_[:], in_=ss_[:], func=AF.Ln), reads=[b_ss], writes=[b_ss])
                        em.op("act", lambda h, ss_=ss_: h.activation(out=ss_[:], in_=ss_[:], func=AF.Exp, scale=-0.5), reads=[b_ss], writes=[b_ss])
                        em.op("dve", lambda h, xo_=xo_, ss_=ss_: h.scalar_tensor_tensor(
                            out=xo_[:], in0=xo_[:], scalar=ss_[:, 0:1], in1=f_bc[:], op0=ALU.mult, op1=ALU.mult),
                            reads=[b_xo, b_ss, b_fbc], writes=[b_xo])
                        em.dma(out_d[r0 - CTX:r0 - CTX + 128, :], xo_[:], reads=[b_xo], writes=[Buf()])
            em.barrier_all()

    def pass_rglru(d):
        NB = 2
        NG = 16 // NB
        with ExitStack() as ps:
            fwd = (d == 0)
            ncol = 4096 if fwd else 2048
            W, b_W = mk(ps, "r_W", [128, KC, ncol], BF16)
            b_W = []
            wa, b_wa = mk(ps, "r_wa", [128, 16, 128], BF16)
            wx, b_wx = mk(ps, "r_wx", [128, 16, 128], BF16)
            for kc in range(KC):
                load_cast(W[:, kc, :], b_W, o_w_in[kc * 128:(kc + 1) * 128, 0:ncol], ncol)
            for (wt, b_wt, src) in ((wa, b_wa, o_w_a), (wx, b_wx, o_w_x)):
                st, b_st = stage[ld_rr[0] % 2]
                ld_rr[0] += 1
                em.dma(st[:, 0:2048].rearrange("p (n j) -> p n j", j=128), src[d].rearrange("n i j -> i n j"), writes=[b_st])
                copy_op(cast_eng(), wt[:].rearrange("p n j -> p (n j)"), st[:, 0:2048], [b_st], [b_wt])
            fr = Front(ps, x1_s, 1)
            H = 3
            xr, _ = mk(ps, "r_xr", [128, 16, NT + H])
            xr_bufs = [Buf("xr%d" % g) for g in range(NG)]
            carry, _ = mk(ps, "r_carry", [128, 16])
            carry_bufs = [Buf("carry%d" % g) for g in range(NG)]
            XC = [mk(ps, "r_xc%d" % i, [128, NB, NT]) for i in range(3)]
            XC = [(t, tuple(Buf("xc_b%d" % nb) for nb in range(NB))) for (t, _) in XC]
            XCB = [mk(ps, "r_xcb%d" % i, [128, NB, NT], BF16) for i in range(2)]
            RR = [mk(ps, "r_r%d" % i, [128, NB, NT]) for i in range(2)]
            II = [mk(ps, "r_i%d" % i, [128, NB, NT]) for i in range(2)]
            AA = [mk(ps, "r_a%d" % i, [128, NB, NT]) for i in range(2)]
            A2 = [mk(ps, "r_a2%d" % i, [128, NB, NT]) for i in range(2)]
            HH = [mk(ps, "r_h%d" % i, [128, NB, NT]) for i in range(2)]
            if fwd:
                HBL = [mk(ps, "r_hbl%d" % i, [128, NB, NT]) for i in range(2)]
                SG = [mk(ps, "r_sg%d" % i, [128, NB, NT]) for i in range(2)]
                y_sb = [mk(ps, "r_y%d" % i, [128, 16, NT], BF16) for i in range(2)]
            d0 = H if fwd else 0
            hl = 0 if fwd else NT
            em.op("pool", lambda h: h.memset(carry[:], 0.0), writes=carry_bufs)
            c1 = DVv("c1_%d" % d)
            c2 = DVv("c2_%d" % d)

            def reg_of(bank, nb):
                return bank[0][:, nb * 256:nb * 256 + NT], bank[1]

            items = []
            for si, (tok0, is_ctx, first) in enumerate(supertiles(reverse=not fwd)):
                stc = dict(si=si, tok0=tok0, is_ctx=is_ctx, first=first, need_out=not is_ctx)
                for g in range(NG):
                    items.append(dict(st=stc, g=g, k=len(items)))

            st_list = []
            for it_ in items:
                if it_["g"] == 0:
                    st_list.append(it_["st"])
            st_list[0]["hT"], st_list[0]["b_hT"] = fr.run(st_list[0]["tok0"], st_list[0]["is_ctx"])

            def stage_A(it):
                stc, g, k = it["st"], it["g"], it["k"]
                if g == 0:
                    if fwd and stc["need_out"]:
                        stc["y"], stc["b_y"] = y_sb[stc["si"] % 2]
                nsi = stc["si"] + 1
                if g == 1 and nsi < len(st_list):
                    fr.load(st_list[nsi]["tok0"])
                hT, b_hT = stc["hT"], stc["b_hT"]
                b_xg = xr_bufs[g]
                n0 = g * NB
                if stc["first"]:
                    em.op("pool", lambda h, n0=n0: h.memset(xr[:, n0:n0 + NB, hl:hl + H], 0.0), writes=[b_xg])
                bank = banks[2 + k % 2]
                for nb in range(NB):
                    reg, b_bk = reg_of(bank, nb)
                    proj_fm(reg, b_bk, W, (n0 + nb) * 128, hT, b_hT, b_W)
                if g == NG - 1 and nsi < len(st_list):
                    st_list[nsi]["hT"], st_list[nsi]["b_hT"] = fr.finish(st_list[nsi]["tok0"], st_list[nsi]["is_ctx"])

            def stage_Ae(it):
                g, k = it["g"], it["k"]
                b_xg = xr_bufs[g]
                n0 = g * NB
                bank = banks[2 + k % 2]
                src = bank[0][:, 0:NB * 256].rearrange("p (n t) -> p n t", n=NB)[:, :, 0:NT]
                copy_op("act", xr[:, n0:n0 + NB, d0:d0 + NT], src, [bank[1]], [b_xg])

            def stage_B(it):
                stc, g, k = it["st"], it["g"], it["k"]
                b_xg = xr_bufs[g]
                n0 = g * NB
                xc, b_xc = XC[k % 3]
                xcb, b_xcb = XCB[k % 2]
                for nb in range(NB):
                    n = n0 + nb
                    row = xr[:, n, :]
                    cw = lambda j, n=n: V("ocw%d%d" % (d, j), n, 1)
                    em.op("pool", lambda h, nb=nb, n=n, row=row, cw=cw, xc=xc: h.tensor_scalar(
                        out=xc[:, nb, :], in0=row[:, d0:d0 + NT], scalar1=cw(3), scalar2=V("ocb%d" % d, n, 1), op0=ALU.mult, op1=ALU.add),
                        reads=[b_xg, b_vecs], writes=[b_xc[nb]])
                for s_ in range(1, 4):
                    for nb in range(NB):
                        n = n0 + nb
                        row = xr[:, n, :]
                        cw = lambda j, n=n: V("ocw%d%d" % (d, j), n, 1)
                        off = d0 - s_ if fwd else d0 + s_
                        em.op("dve", lambda h, nb=nb, row=row, cw=cw, s_=s_, off=off, xc=xc: h.scalar_tensor_tensor(
                            out=xc[:, nb, :], in0=row[:, off:off + NT], scalar=cw(3 - s_), in1=xc[:, nb, :], op0=ALU.mult, op1=ALU.add),
                            reads=[b_xg, b_xc[nb], b_vecs], writes=[b_xc[nb]])
                src0 = (d0 + NT - H) if fwd else d0
                em.op("pool", lambda h, n0=n0, src0=src0: h.tensor_copy(out=xr[:, n0:n0 + NB, hl:hl + H], in_=xr[:, n0:n0 + NB, src0:src0 + H]),
                      reads=[b_xg, b_xc], writes=[b_xg])
                em.op("pool", lambda h, xc=xc, xcb=xcb: h.tensor_copy(out=xcb[:], in_=xc[:]), reads=[b_xc], writes=[b_xcb])

            def stage_C(it):
                stc, g, k = it["st"], it["g"], it["k"]
                n0 = g * NB
                xcb, b_xcb = XCB[k % 2]
                r_, b_r = RR[k % 2]
                i_, b_i = II[k % 2]
                bank_r = banks[4 + k % 2]
                bank_i = banks[6 + k % 2]
                for nb in range(NB):
                    n = n0 + nb
                    reg, b_bk = reg_of(bank_r, nb)
                    em.op("pe", lambda h, reg=reg, n=n, nb=nb, xcb=xcb: h.matmul(reg, lhsT=wa[:, n, :], rhs=xcb[:, nb, :], start=True, stop=True),
                          reads=[b_wa, b_xcb], writes=[b_bk])
                for nb in range(NB):
                    n = n0 + nb
                    reg2, b_bk2 = reg_of(bank_i, nb)
                    em.op("pe", lambda h, reg2=reg2, n=n, nb=nb, xcb=xcb: h.matmul(reg2, lhsT=wx[:, n, :], rhs=xcb[:, nb, :], start=True, stop=True),
                          reads=[b_wx, b_xcb], writes=[b_bk2])
                for nb in range(NB):
                    n = n0 + nb
                    reg, b_bk = reg_of(bank_r, nb)
                    em.op("act", lambda h, reg=reg, n=n, nb=nb, r_=r_: h.activation(out=r_[:, nb, :], in_=reg, func=AF.Sigmoid, bias=V("oba%d" % d, n, 1)),
                          reads=[b_bk, b_vecs], writes=[b_r])
                for nb in range(NB):
                    n = n0 + nb
                    reg2, b_bk2 = reg_of(bank_i, nb)
                    em.op("act", lambda h, reg2=reg2, n=n, nb=nb, i_=i_: h.activation(out=i_[:, nb, :], in_=reg2, func=AF.Sigmoid, bias=V("obx%d" % d, n, 1)),
                          reads=[b_bk2, b_vecs], writes=[b_i])
                if fwd and stc["need_out"]:
                    hbl, b_hbl = HBL[k % 2]
                    em.dma(hbl[:], hb_s[n0:n0 + NB, :, stc["tok0"]:stc["tok0"] + NT].rearrange("c p t -> p c t"), writes=[b_hbl])
                    bank = banks[2 + (k + 1) % 2]
                    for nb in range(NB):
                        reg, b_bk = reg_of(bank, nb)
                        proj_fm(reg, b_bk, W, 2048 + (n0 + nb) * 128, stc["hT"], stc["b_hT"], b_W)

            def stage_D(it):
                stc, g, k = it["st"], it["g"], it["k"]
                n0 = g * NB
                xc, b_xc = XC[k % 3]
                r_, b_r = RR[k % 2]
                i_, b_i = II[k % 2]
                a_, b_a = AA[k % 2]
                a2, b_a2 = A2[k % 2]
                hh, b_hh = HH[k % 2]
                b_cg = carry_bufs[g]
                out_fwd = fwd and stc["need_out"]
                if out_fwd:
                    sg, b_sg = SG[k % 2]
                    gbank = banks[2 + (k + 1) % 2]
                    gsrc = gbank[0][:, 0:NB * 256].rearrange("p (n t) -> p n t", n=NB)[:, :, 0:NT]
                    em.op("act", lambda h, sg=sg, gsrc=gsrc: h.activation(out=sg[:], in_=gsrc, func=AF.Sigmoid), reads=[gbank[1]], writes=[b_sg])
                    em.op("dve", lambda h, sg=sg, gsrc=gsrc: h.tensor_tensor(out=sg[:], in0=gsrc, in1=sg[:], op=ALU.mult), reads=[gbank[1], b_sg], writes=[b_sg])
                for nb in range(NB):
                    n = n0 + nb
                    em.op("act", lambda h, n=n, nb=nb, a_=a_, r_=r_: h.activation(out=a_[:, nb, :], in_=r_[:, nb, :], func=AF.Exp, scale=c1[:, n:n + 1]),
                          reads=[b_r, b_dv], writes=[b_a])
                    em.op("act", lambda h, n=n, nb=nb, a2=a2, r_=r_: h.activation(out=a2[:, nb, :], in_=r_[:, nb, :], func=AF.Exp, scale=c2[:, n:n + 1]),
                          reads=[b_r, b_dv], writes=[b_a2])
                em.op("act", lambda h, a2=a2: h.activation(out=a2[:], in_=a2[:], func=AF.Ln, scale=-1.0, bias=1.0), reads=[b_a2], writes=[b_a2])
                em.op("act", lambda h, a2=a2: h.activation(out=a2[:], in_=a2[:], func=AF.Exp, scale=0.5), reads=[b_a2], writes=[b_a2])
                em.op("pool", lambda h, i_=i_, xc=xc: h.tensor_tensor(out=i_[:], in0=i_[:], in1=xc[:], op=ALU.mult), reads=[b_i, b_xc], writes=[b_i])
                em.op("dve", lambda h, i_=i_, a2=a2: h.tensor_tensor(out=i_[:], in0=i_[:], in1=a2[:], op=ALU.mult), reads=[b_i, b_a2], writes=[b_i])
                for nb in range(NB):
                    n = n0 + nb
                    if fwd:
                        em.op("dve", lambda h, n=n, nb=nb, hh=hh, a_=a_, i_=i_: h.tensor_tensor_scan(
                            out=hh[:, nb, :], data0=a_[:, nb, :], data1=i_[:, nb, :], initial=carry[:, n:n + 1], op0=ALU.mult, op1=ALU.add),
                            reads=[b_a, b_i, b_cg], writes=[b_hh])
                    else:
                        em.op("dve", lambda h, n=n, nb=nb, hh=hh, a_=a_, i_=i_: h.tensor_tensor_scan(
                            out=hh[:, nb, ::-1], data0=a_[:, nb, ::-1], data1=i_[:, nb, ::-1], initial=carry[:, n:n + 1],
                            op0=ALU.mult, op1=ALU.add), reads=[b_a, b_i, b_cg], writes=[b_hh])
                lc = NT - 1 if fwd else 0
                em.op("pool", lambda h, n0=n0, lc=lc, hh=hh: h.tensor_copy(out=carry[:, n0:n0 + NB], in_=hh[:, :, lc]), reads=[b_hh], writes=[b_cg])
                if not stc["need_out"]:
                    return
                tok0 = stc["tok0"]
                if not fwd:
                    em.dma(hb_s[n0:n0 + NB, :, tok0:tok0 + NT].rearrange("c p t -> p c t"), hh[:], reads=[b_hh], writes=[Buf()])
                    return
                hbl, b_hbl = HBL[k % 2]
                y, b_y = stc["y"], stc["b_y"]
                em.op("pool", lambda h, hh=hh, hbl=hbl: h.tensor_tensor(out=hh[:], in0=hh[:], in1=hbl[:], op=ALU.add), reads=[b_hh, b_hbl], writes=[b_hh])
                em.op("pool", lambda h, n0=n0, y=y, hh=hh, sg=sg: h.tensor_tensor(out=y[:, n0:n0 + NB, :], in0=hh[:], in1=sg[:], op=ALU.mult),
                      reads=[b_hh, b_sg], writes=[b_y])
                if g == NG - 1:
                    em.dma(in1_s[:, :, tok0:tok0 + NT].rearrange("c p t -> p c t"), y[:], reads=[b_y], writes=[Buf()])

            order = ((stage_Ae, 1), (stage_D, 3), (stage_A, 0), (stage_B, 1), (stage_C, 2))
            for step in range(len(items) + 3):
                for fn, lag in order:
                    idx = step - lag
                    if 0 <= idx < len(items):
                        fn(items[idx])
            em.barrier_all()

    em.barrier_all()
    plist = [lambda: pass_gla(1), lambda: pass_gla(0), pass_sconv, lambda: pass_out(0),
             lambda: pass_rglru(1), lambda: pass_rglru(0), lambda: pass_out(1)]
    for pi_, p_ in enumerate(plist[:npass]):
        em.mark("PASS%d" % pi_)
        p_()
    em.mark("END")
    em.finish()
    gs.close()
    return nc, em


_CACHE = {}


def make_in_maps(inp, n_cores=8):
    B = inp["x"].shape[0]
    maps = []
    shared = {
        "w_mod": np.ascontiguousarray(inp["w_mod"], np.float32),
        "e_w_in": np.ascontiguousarray(inp["e_w_in"][0], np.float32),
        "e_w_a2": np.ascontiguousarray(inp["e_w_a2"][0], np.float32),
        "e_w_out": np.ascontiguousarray(inp["e_w_out"][0], np.float32),
        "o_w_in": np.ascontiguousarray(inp["o_w_in"][0], np.float32),
        "o_w_a": np.ascontiguousarray(inp["o_w_a"][0], np.float32),
        "o_w_x": np.ascontiguousarray(inp["o_w_x"][0], np.float32),
        "o_w_out": np.ascontiguousarray(inp["o_w_out"][0], np.float32),
    }
    for core in range(n_cores):
        b = core % B
        m = dict(shared)
        m["xin"] = np.ascontiguousarray(np.concatenate([inp["ctx"][b], inp["x"][b]], axis=0), np.float32)
        m["vecs"] = pack_vecs(b, inp)
        maps.append(m)
    return maps


def kernel(**inputs):
    inp = {k: np.asarray(v) for k, v in inputs.items()}
    B, L, _ = inp["x"].shape
    CTX = inp["ctx"].shape[1]
    key = (L, CTX)
    if key not in _CACHE:
        _CACHE[key] = build(L, CTX)[0]
    nc = _CACHE[key]
    maps = make_in_maps(inp, 8)
    res = run_bass_kernel_spmd(nc, maps, core_ids=list(range(8)))
    out = np.stack([np.asarray(res.results[b]["out"], np.float32) for b in range(B)], axis=0)
    return out
```
